# Optimizing a Trainium2 kernel written in Bass

```python
import jax, jax.numpy as jnp
from jax import lax
import numpy as np

D_MODEL = 1024
BATCH = 16
SEQ = 4096
DEPTH = 4

CHUNK = 128
A_WIDTH = D_MODEL
A_GROUPS = 8
A_GROUP_DIM = A_WIDTH // A_GROUPS
HEAD_DIM = 64
N_HEADS = D_MODEL // HEAD_DIM
N_KV = 4
GQ = N_HEADS // N_KV
Q_WIDTH = N_HEADS * HEAD_DIM
KV_WIDTH = N_KV * HEAD_DIM
BLOCK = 128
WINDOW = 128
D_FF = 4 * D_MODEL
EPS = 1e-6
SECTION_SIZES = [D_MODEL, D_MODEL, 2 * A_WIDTH, Q_WIDTH, KV_WIDTH, KV_WIDTH]
SPLITS = [int(s) for s in np.cumsum(SECTION_SIZES)[:-1]]
IN_WIDTH = int(sum(SECTION_SIZES))

kernel_name = "hybrid_gmlp_swa_alibi_encoder"


def _rmsnorm(x, g):
    xf = x.astype(jnp.float32)
    y = xf * lax.rsqrt(jnp.mean(xf * xf, axis=-1, keepdims=True) + EPS)
    return (y * g.astype(jnp.float32)).astype(x.dtype)


def _spatial_gating(za, g_v, w_s, b_s):
    B, S, _ = za.shape
    za = jax.nn.gelu(za, approximate=False)
    u, v = jnp.split(za, 2, axis=-1)
    v = _rmsnorm(v, g_v)
    nc = S // CHUNK
    v = v.reshape(B, nc, CHUNK, A_GROUPS, A_GROUP_DIM)
    s = jnp.einsum('gpq,bnqgc->bnpgc', w_s, v) + b_s.T[None, None, :, :, None]
    return u * s.reshape(B, S, A_WIDTH)


def _window_attention(q, k, v, g_q, g_k, sink):
    B, S, _ = q.shape
    nb = S // BLOCK
    f32 = jnp.float32
    q = _rmsnorm(q.reshape(B, S, N_HEADS, HEAD_DIM), g_q)
    k = _rmsnorm(k.reshape(B, S, N_KV, HEAD_DIM), g_k)
    v = v.reshape(B, S, N_KV, HEAD_DIM)
    qb = q.reshape(B, nb, BLOCK, N_KV, GQ, HEAD_DIM).transpose(1, 0, 2, 3, 4, 5)

    def band(t):
        tb = t.reshape(B, nb, BLOCK, N_KV, HEAD_DIM)
        tp = jnp.pad(tb, ((0, 0), (1, 1), (0, 0), (0, 0), (0, 0)))
        w = jnp.concatenate([tp[:, :-2], tp[:, 1:-1], tp[:, 2:]], axis=2)
        return w.transpose(1, 0, 2, 3, 4)

    kw, vw = band(k), band(v)
    qi = jnp.arange(BLOCK)[:, None]
    kj = jnp.arange(3 * BLOCK)[None, :]
    rel = kj - BLOCK - qi
    in_window = jnp.abs(rel) <= WINDOW
    slopes = jnp.exp2(-8.0 * (jnp.arange(N_HEADS, dtype=f32) + 1.0) / N_HEADS).reshape(N_KV, GQ)
    alibi = -slopes[:, :, None, None] * jnp.abs(rel).astype(f32)[None, None]
    sink_f = sink.astype(f32).reshape(N_KV, GQ)[None, :, :, None]
    scale = HEAD_DIM ** -0.5
    neg = jnp.float32(-1e30)

    def one_block(args):
        qblk, kblk, vblk, i = args
        s = jnp.einsum('bqkgd,bskd->bkgqs', qblk.astype(f32), kblk.astype(f32)) * scale + alibi
        kpos = (i - 1) * BLOCK + kj
        valid = in_window & (kpos >= 0) & (kpos < S)
        s = jnp.where(valid, s, neg)
        m = jnp.maximum(jnp.max(s, axis=-1), sink_f)
        p = jnp.exp(s - m[..., None])
        denom = jnp.sum(p, axis=-1) + jnp.exp(sink_f - m)
        o = jnp.einsum('bkgqs,bskd->bqkgd', p, vblk.astype(f32))
        o = o / denom.transpose(0, 3, 1, 2)[..., None]
        return o.astype(qblk.dtype)

    o = lax.map(one_block, (qb, kw, vw, jnp.arange(nb)))
    return o.transpose(1, 0, 2, 3, 4, 5).reshape(B, S, Q_WIDTH)


def setup_inputs(seed: int = 0) -> dict:
    key = jax.random.key(seed)
    ks = jax.random.split(key, 16)
    L = DEPTH
    nrm = lambda k, shape, fan: jax.random.normal(k, shape, jnp.float32) * (fan ** -0.5)
    gain = lambda k, shape: 1.0 + 0.02 * jax.random.normal(k, shape, jnp.float32)
    return {
        "x": jax.random.normal(ks[0], (BATCH, SEQ, D_MODEL), jnp.float32),
        "ln1_g": gain(ks[1], (L, D_MODEL)),
        "w_in": nrm(ks[2], (L, D_MODEL, IN_WIDTH), D_MODEL),
        "a_norm_g": gain(ks[3], (L, A_WIDTH)),
        "a_w_s": nrm(ks[4], (L, A_GROUPS, CHUNK, CHUNK), CHUNK),
        "a_b_s": 1.0 + 0.01 * jax.random.normal(ks[5], (L, A_GROUPS, CHUNK), jnp.float32),
        "b_q_norm_g": gain(ks[6], (L, HEAD_DIM)),
        "b_k_norm_g": gain(ks[7], (L, HEAD_DIM)),
        "b_sink": 0.5 * jax.random.normal(ks[8], (L, N_HEADS), jnp.float32),
        "w_branch_a": nrm(ks[9], (L, A_WIDTH, D_MODEL), A_WIDTH),
        "w_branch_b": nrm(ks[10], (L, Q_WIDTH, D_MODEL), Q_WIDTH),
        "w_out": nrm(ks[11], (L, D_MODEL, D_MODEL), D_MODEL),
        "ln2_g": gain(ks[12], (L, D_MODEL)),
        "w_ff1": nrm(ks[13], (L, D_MODEL, D_FF), D_MODEL),
        "w_ff2": nrm(ks[14], (L, D_FF, D_MODEL), D_FF),
    }


def reference(x, ln1_g, w_in, a_norm_g, a_w_s, a_b_s, b_q_norm_g, b_k_norm_g, b_sink,
              w_branch_a, w_branch_b, w_out, ln2_g, w_ff1, w_ff2):
    for l in range(DEPTH):
        h = _rmsnorm(x, ln1_g[l])
        z = h @ w_in[l]
        g_a, g_b, za, q, k, v = jnp.split(z, SPLITS, axis=-1)
        y_a = _spatial_gating(za, a_norm_g[l], a_w_s[l], a_b_s[l]) @ w_branch_a[l]
        y_b = _window_attention(q, k, v, b_q_norm_g[l], b_k_norm_g[l], b_sink[l]) @ w_branch_b[l]
        mixed = jax.nn.sigmoid(g_a) * y_a + jax.nn.sigmoid(g_b) * y_b
        x = x + mixed @ w_out[l]
        h = _rmsnorm(x, ln2_g[l])
        x = x + jnp.square(jax.nn.relu(h @ w_ff1[l])) @ w_ff2[l]
    return x
```

```python
import contextlib
import numpy as np
import concourse.bass as bass
import concourse.mybir as mybir
from concourse.ap import AP
from concourse.bass_utils import run_bass_kernel_spmd

F32 = mybir.dt.float32
BF16 = mybir.dt.bfloat16
AF = mybir.ActivationFunctionType
ALU = mybir.AluOpType
AX = mybir.AxisListType

D = 1024
KC = 8
TT = 512
NBLK = 4
HD = 64
NH = 16
NKV = 4
GQ = 4
DFF = 4096
INW = 5632
EPS = 1e-6
ATOM = 512
NSLOT = 4
PREF = 2


class Sched:
    def __init__(self):
        self.ops = []
        self.lastw = {}
        self.lastr = {}
        self.dma_count = {}
        self.stopped = False

    def add(self, eng, fn, r=(), w=(), dma=None, bulk=False):
        if self.stopped:
            return -1
        i = len(self.ops)
        deps = set()
        lastw, lastr = self.lastw, self.lastr
        for a in r:
            j = lastw.get(a)
            if j is not None:
                deps.add(j)
        for a in w:
            j = lastw.get(a)
            if j is not None:
                deps.add(j)
            rd = lastr.get(a)
            if rd:
                deps.update(rd.values())
        ekey = ("d", dma) if dma else eng
        for a in w:
            lastw[a] = i
            lastr[a] = {}
        for a in r:
            d = lastr.get(a)
            if d is None:
                d = lastr[a] = {}
            d[ekey] = i
        deps.discard(i)
        o = 0
        if dma:
            o = self.dma_count[dma] = self.dma_count.get(dma, 0) + 1
        self.ops.append([eng, fn, deps, dma, o, bulk, False, 0, None, None])
        return i

    def finalize(self):
        ops = self.ops
        for op in ops:
            eng = op[0]
            nd = set()
            for j in op[2]:
                y = ops[j]
                if y[3] is None and y[0] == eng and eng in ("pe", "sp"):
                    continue
                nd.add(j)
                if y[3] is None:
                    y[6] = True
            op[2] = nd
        cnt = {}
        for op in ops:
            if op[3] is None and op[6]:
                cnt[op[0]] = cnt.get(op[0], 0) + 1
                op[7] = cnt[op[0]]
        vc = {}
        nwait = 0
        for op in ops:
            e = op[0]
            my = vc.setdefault(e, {})
            waits = {}
            for j in op[2]:
                y = ops[j]
                if y[3] is not None:
                    key = ("d", y[3])
                    val = 16 * (self.dma_count[y[3]] if y[5] else y[4])
                else:
                    key = ("e", y[0])
                    val = y[7]
                if my.get(key, 0) >= val:
                    continue
                if waits.get(key, 0) < val:
                    waits[key] = val
            for key, val in waits.items():
                my[key] = val
            for j in op[2]:
                s = ops[j][9]
                if s:
                    for k, v in s.items():
                        if my.get(k, 0) < v:
                            my[k] = v
            op[8] = list(waits.items())
            nwait += len(op[8])
            if op[3] is not None:
                snap = dict(my)
                if not op[5]:
                    snap[("d", op[3])] = max(snap.get(("d", op[3]), 0), 16 * op[4])
                op[9] = snap
            elif op[6]:
                snap = dict(my)
                snap[("e", e)] = op[7]
                op[9] = snap
        self.nwait = nwait

    def emit_engine(self, eng, h, esem, dsem):
        def semof(key):
            return dsem[key[1]] if key[0] == "d" else esem[key[1]]

        n = 0
        for op in self.ops:
            if op[0] != eng:
                continue
            waits = op[8]
            fn = op[1]
            if fn is None or op[3] is not None:
                for key, val in waits:
                    h.wait_ge(semof(key), val)
                    n += 1
                if fn is None:
                    continue
                ins = fn(h)
            else:
                for key, val in waits[1:]:
                    h.wait_ge(semof(key), val)
                    n += 1
                ins = fn(h)
                if waits:
                    ins._wait_ge(semof(waits[0][0]), waits[0][1])
            n += 1
            if op[3] is not None:
                ins.then_inc(dsem[op[3]], 16)
            elif op[6]:
                ins.then_inc(esem[eng], 1)
        return n


class Acc:
    __slots__ = ("ap", "atoms")

    def __init__(self, ap, atoms):
        self.ap = ap
        self.atoms = atoms


class Buf:
    def __init__(self, name, t, fs, esz, track=None, base=0, pstep=None, eoff=0, atom=ATOM):
        self.name = name
        self.t = t
        self.fs = tuple(fs)
        self.esz = esz
        st = []
        s = 1
        for n in reversed(self.fs):
            st.append(s)
            s *= n
        self.st = tuple(reversed(st))
        self.pstep = pstep or s
        self.eoff = eoff
        self.track = track or name
        self.base = base
        self.cache = {}
        self.atom = atom

    def v(self, *idx, p=(0, 128)):
        key = (idx, p)
        c = self.cache.get(key)
        if c is not None:
            return c
        assert len(idx) <= len(self.fs)
        idx = tuple(idx) + (None,) * (len(self.fs) - len(idx))
        off = p[0] * self.pstep + self.eoff
        dims = [[self.pstep, p[1] - p[0]]]
        rng = []
        for i, ix in enumerate(idx):
            if ix is None:
                lo, hi = 0, self.fs[i]
            elif isinstance(ix, tuple):
                lo, hi = ix
            else:
                off += ix * self.st[i]
                rng.append((self.st[i], ix, ix + 1))
                continue
            assert 0 <= lo < hi <= self.fs[i], (self.name, idx)
            off += lo * self.st[i]
            dims.append([self.st[i], hi - lo])
            rng.append((self.st[i], lo, hi))
        ap = AP(self.t, off, dims)
        atoms = self._atoms(rng)
        c = Acc(ap, atoms)
        self.cache[key] = c
        return c

    def _atoms(self, rng):
        runs = [(0,)]
        starts = [0]
        for (st, lo, hi) in rng[:-1]:
            starts = [s + st * k for s in starts for k in range(lo, hi)]
        st, lo, hi = rng[-1]
        atoms = set()
        for s in starts:
            b0 = self.base + (s + st * lo) * self.esz
            b1 = self.base + (s + st * (hi - 1) + 1) * self.esz
            for a in range(b0 // self.atom, (b1 - 1) // self.atom + 1):
                atoms.add((self.track, a))
        return frozenset(atoms)

    def custom(self, off_elems, dims, p=(0, 128), span=None):
        off = p[0] * self.pstep + self.eoff + off_elems
        ap = AP(self.t, off, [[self.pstep, p[1] - p[0]]] + [list(d) for d in dims])
        if span is None:
            span = 1 + sum(s * (n - 1) for s, n in dims)
        b0 = self.base + off_elems * self.esz
        b1 = self.base + (off_elems + span) * self.esz
        atoms = frozenset((self.track, a) for a in range(b0 // self.atom, (b1 - 1) // self.atom + 1))
        return Acc(ap, atoms)


class Ring:
    def __init__(self, bufs):
        self.bufs = bufs
        self.i = 0

    def next(self):
        b = self.bufs[self.i % len(self.bufs)]
        self.i += 1
        return b


SB_LO = 16512
SB_HI = 229344


def build_program(nseq, S, L):
    NTOK = nseq * S
    NT = NTOK // TT
    TPS = S // TT
    NBT = NTOK // 128
    nc = bass.Bass("TRN2", target_bir_lowering=False)
    sc = Sched()
    import os as _os
    kstop = float(_os.environ.get("KSTOP", "99"))

    def ck(n):
        if n >= kstop:
            sc.stopped = True
    es = contextlib.ExitStack()

    def dram(name, shape, dt, kind):
        return nc.dram_tensor(name, shape, dt, kind=kind).ap()

    x_in = dram("x", [NTOK, D], F32, "ExternalInput")
    y_out = dram("y", [NTOK, D], F32, "ExternalOutput")
    w_in = dram("w_in", [L, D, INW], F32, "ExternalInput")
    w_a = dram("w_branch_a", [L, D, D], F32, "ExternalInput")
    w_b = dram("w_branch_b", [L, D, D], F32, "ExternalInput")
    w_o = dram("w_out", [L, D, D], F32, "ExternalInput")
    w_f1 = dram("w_ff1", [L, D, DFF], F32, "ExternalInput")
    w_f2 = dram("w_ff2", [L, DFF, D], F32, "ExternalInput")
    a_ws = dram("a_w_s", [L, 8, 128, 128], F32, "ExternalInput")
    a_bs = dram("a_b_s", [L, 1024], F32, "ExternalInput")
    a_gv = dram("a_norm_g", [L, 1024], F32, "ExternalInput")
    sink = dram("b_sink", [1, L * 16], F32, "ExternalInput")
    g1c = dram("g1col", [128, L * 8], F32, "ExternalInput")
    g2c = dram("g2col", [128, L * 8], F32, "ExternalInput")
    gqc = dram("gqcol", [128, L], F32, "ExternalInput")
    gkc = dram("gkcol", [128, L], F32, "ExternalInput")
    c_ident = dram("c_ident", [128, 128], F32, "ExternalInput")
    c_mtab = dram("c_mtab", [128, NKV * 3 * 512], F32, "ExternalInput")
    xs = [dram(f"xs{i}", [KC, 128, NTOK], F32, "Internal") for i in range(2)]
    win_bf = dram("win_bf", [L, D, INW], BF16, "Internal")
    wa_bf = dram("wa_bf", [L, D, D], BF16, "Internal")
    wb_bf = dram("wb_bf", [L, D, D], BF16, "Internal")
    wo_bf = dram("wo_bf", [L, D, D], BF16, "Internal")
    f1_bf = dram("f1_bf", [L, D, DFF], BF16, "Internal")
    f2_bf = dram("f2_bf", [L, DFF, D], BF16, "Internal")

    cur = [SB_LO]

    def sb(name, fs, dt, at=None):
        esz = 2 if dt == BF16 else 4
        n = 1
        for k in fs:
            n *= k
        nbytes = n * esz
        if at is None:
            at = cur[0]
            cur[0] = (at + nbytes + 31) // 32 * 32
            assert cur[0] <= SB_HI, ("SBUF overflow", name, cur[0])
        t = nc.alloc_sbuf_tensor_at(name, [128] + list(fs), dt, offset=at)
        return Buf(name, t, fs, esz, track="SB", base=at)

    xT = [sb(f"xT{i}", [KC, 768], F32) for i in range(2)]
    hT = sb("hT", [KC, 768], BF16)
    sq = sb("sq", [KC, 768], BF16)
    rs = sb("rs", [768], F32)
    sga = sb("sga", [KC, TT], BF16)
    sgb = sb("sgb", [KC, TT], BF16)
    um = sb("um", [KC, TT], BF16)
    OT = sb("OT", [KC, TT], BF16, at=hT.base)
    h2T = Buf("h2T", sga.t, [KC, TT], 2, track="SB", base=sga.base)
    wkd = sb("wkd", [KC, NKV, 128], BF16, at=sq.base)
    A0 = cur[0]
    cur[0] += 32768
    actT = sb("actT", [32, TT], BF16, at=A0)
    QT = sb("QT", [KC, TT], BF16, at=A0)
    KT = sb("KT", [NKV, 2, 768], BF16, at=A0 + 8192)
    VD = sb("VD", [6, NKV, 128], BF16, at=A0 + 20480)
    vpp = sb("vpp", [NBLK, 1024], BF16, at=A0 + 24576)
    gvb = Ring([sb("gvb0", [1024], F32, at=A0), sb("gvb1", [1024], F32, at=A0 + 4096)])
    junk = sb("junk", [1024], F32, at=A0 + 8192)
    mixT = sb("mixT", [KC, TT], BF16, at=A0)
    wsraw = sb("wsraw", [8, 128], F32, at=A0 + 20480)
    bs32 = sb("bs32", [1024], F32, at=A0)
    bhi32 = sb("bhi32", [1024], F32, at=A0 + 4096)
    blo32 = sb("blo32", [1024], F32, at=A0 + 8192)
    bhi = sb("bhi", [1024], BF16, at=A0 + 12288)
    xinr = Ring([sb("xin0", [1024], F32, at=A0), sb("xin1", [1024], F32, at=A0 + 4096)])
    xtsr = Ring([sb("xts0", [KC, 128], F32, at=A0 + 8192), sb("xts1", [KC, 128], F32, at=A0 + 12288)])
    f32r = Ring([sb(f"f32r{i}", [TT], F32) for i in range(4)])
    bfr = Ring([sb(f"bfr{i}", [TT], BF16) for i in range(10)])
    wslots = [sb(f"wsl{i}", [KC, 512], BF16) for i in range(NSLOT)]
    MT = sb("MT", [NKV, 3, 512], BF16)
    esx = sb("esx", [NKV, 512], BF16)
    eh = sb("eh", [16], BF16)
    eh32 = sb("eh32", [16], F32)
    el32 = sb("el32", [16], F32)
    gvbc = sb("gvbc", [1024], F32)
    WsT = sb("WsT", [8, 128], BF16)
    bsx = sb("bsx", [1024], BF16)
    ident = sb("ident", [128], F32)
    ones_bf = sb("ones_bf", [128], BF16)
    bd_bf = sb("bd_bf", [128], BF16)
    ones33 = sb("ones33", [128], BF16)
    g1col = sb("g1col_s", [L * 8], F32)
    g2col = sb("g2col_s", [L * 8], F32)
    gqcol = sb("gqcol_s", [L], F32)
    gkcol = sb("gkcol_s", [L], F32)
    eskraw = sb("eskraw", [L * 16], F32)
    esk = sb("esk", [L * 16], F32)
    ssb = sb("ssb", [8], F32)
    eps_col = sb("eps_col", [1], F32)

    pst = es.enter_context(nc.psum_tensor("ps", [128, 8, 512], F32))
    PS = Buf("ps", pst, [8, 512], 4, track="PS", atom=2048)
    bank_ctr = [0]

    def bank():
        b = bank_ctr[0] % 8
        bank_ctr[0] += 1
        return b

    def mm(out, lhsT, rhs, start, stop):
        sc.add("pe", lambda e, o=out.ap, l=lhsT.ap, r=rhs.ap, s0=start, s1=stop:
               e.matmul(o, lhsT=l, rhs=r, start=s0, stop=s1),
               r=lhsT.atoms | rhs.atoms, w=out.atoms)

    def act(out, in_, func, scale=1.0, bias=None):
        if bias is None:
            sc.add("act", lambda e, o=out.ap, i=in_.ap, f=func, s=scale:
                   e.activation(out=o, in_=i, func=f, scale=s),
                   r=in_.atoms, w=out.atoms)
        else:
            sc.add("act", lambda e, o=out.ap, i=in_.ap, f=func, s=scale, b=bias.ap:
                   e.activation(out=o, in_=i, func=f, scale=s, bias=b),
                   r=in_.atoms | bias.atoms, w=out.atoms)

    def tt(eng, out, in0, in1, op):
        sc.add(eng, lambda e, o=out.ap, a=in0.ap, b=in1.ap, op=op:
               e.tensor_tensor(out=o, in0=a, in1=b, op=op),
               r=in0.atoms | in1.atoms, w=out.atoms)

    def stt(eng, out, in0, scalar, in1, op0, op1):
        if isinstance(scalar, Acc):
            sc.add(eng, lambda e, o=out.ap, a=in0.ap, s=scalar.ap, b=in1.ap, op0=op0, op1=op1:
                   e.scalar_tensor_tensor(out=o, in0=a, scalar=s, in1=b, op0=op0, op1=op1),
                   r=in0.atoms | in1.atoms | scalar.atoms, w=out.atoms)
        else:
            sc.add(eng, lambda e, o=out.ap, a=in0.ap, s=float(scalar), b=in1.ap, op0=op0, op1=op1:
                   e.scalar_tensor_tensor(out=o, in0=a, scalar=s, in1=b, op0=op0, op1=op1),
                   r=in0.atoms | in1.atoms, w=out.atoms)

    def recip(eng, out, in_):
        sc.add(eng, lambda e, o=out.ap, i=in_.ap: e.reciprocal(out=o, in_=i),
               r=in_.atoms, w=out.atoms)

    def copy(eng, out, in_):
        if eng == "act":
            sc.add("act", lambda e, o=out.ap, i=in_.ap: e.copy(out=o, in_=i), r=in_.atoms, w=out.atoms)
        else:
            sc.add(eng, lambda e, o=out.ap, i=in_.ap: e.tensor_copy(out=o, in_=i), r=in_.atoms, w=out.atoms)

    def memset(eng, out, val):
        sc.add(eng, lambda e, o=out.ap, v=val: e.memset(o, v), w=out.atoms)

    def dma(eng, out_ap, in_ap, r, w, sem, bulk=False):
        sc.add(eng, lambda e, o=out_ap, i=in_ap: e.dma_start(out=o, in_=i), r=r, w=w, dma=sem, bulk=bulk)

    def transpose(out, in_):
        sc.add("pe", lambda e, o=out.ap, i=in_.ap, idn=ident.v().ap: e.transpose(o, i, idn),
               r=in_.atoms | ident.v().atoms, w=out.atoms)

    def rstd(out, in_, scale):
        act(out, in_, AF.Ln, scale=scale, bias=eps_col.v())
        act(out, out, AF.Exp, scale=-0.5)

    memset("dve", eps_col.v(), EPS)
    memset("dve", ones_bf.v(), 1.0)
    memset("dve", bd_bf.v(), 0.0)
    memset("dve", bd_bf.v((0, 64), p=(0, 64)), 1.0)
    memset("dve", bd_bf.v((64, 128), p=(64, 128)), 1.0)
    memset("dve", ones33.v(), 0.0)
    memset("dve", ones33.v(p=(0, 1)), 1.0)
    memset("dve", ones33.v(p=(32, 33)), 1.0)
    dma("sp", ident.v().ap, c_ident, [], ident.v().atoms, "c0")
    dma("sp", g1col.v().ap, g1c, [], g1col.v().atoms, "c2")
    dma("sp", g2col.v().ap, g2c, [], g2col.v().atoms, "c3")
    dma("sp", gqcol.v().ap, gqc, [], gqcol.v().atoms, "c4")
    dma("sp", gkcol.v().ap, gkc, [], gkcol.v().atoms, "c5")
    dma("sp", eskraw.v().ap, sink.partition_broadcast(128).rearrange("p a b -> p (a b)"), [], eskraw.v().atoms, "c6")
    act(esk.v(), eskraw.v(), AF.Exp)
    ck(1)

    NCS = 3
    CW = 2048
    cst_in = [sb(f"cst_in{i}", [CW], F32, at=xT[0].base + i * 12288) for i in range(NCS)]
    cst_out = [sb(f"cst_out{i}", [CW], BF16, at=xT[0].base + i * 12288 + 8192) for i in range(NCS)]
    for kh in range(NKV):
        a = cst_in[kh % NCS]
        dma("sp", a.v((0, 1536)).ap, c_mtab[:, kh * 1536:(kh + 1) * 1536], [], a.v((0, 1536)).atoms, f"cin{kh % NCS}")
        copy("dve", MT.custom(kh * 1536, [[1, 1536]]), a.v((0, 1536)))
    castkeys = {}
    cctr = [NKV]
    for l in range(L):
        ckl = castkeys[l] = []

        def cast2d(dst, src, rows, cols, ckl=ckl, l=l):
            for r0 in range(0, rows, 128):
                for c0 in range(0, cols, CW):
                    cw = min(CW, cols - c0)
                    i = cctr[0] % NCS
                    cctr[0] += 1
                    a, b = cst_in[i], cst_out[i]
                    dma("sp", a.v((0, cw)).ap, src[r0:r0 + 128, c0:c0 + cw], [], a.v((0, cw)).atoms, f"cin{i}")
                    copy(("dve", "act", "pool")[i], b.v((0, cw)), a.v((0, cw)))
                    k = ("wbf", l, len(ckl))
                    ckl.append(k)
                    dma("sp", dst[r0:r0 + 128, c0:c0 + cw], b.v((0, cw)).ap, b.v((0, cw)).atoms, [k], f"cout{i}")

        cast2d(win_bf[l], w_in[l], D, INW)
        cast2d(wa_bf[l], w_a[l], D, D)
        cast2d(wb_bf[l], w_b[l], D, D)
        cast2d(wo_bf[l], w_o[l], D, D)
        cast2d(f1_bf[l], w_f1[l], D, DFF)
        cast2d(f2_bf[l], w_f2[l], DFF, D)

    ck(2)
    for blk in range(NBT):
        xi = xinr.next()
        dma("sp", xi.v().ap, x_in[blk * 128:(blk + 1) * 128, :], [], xi.v().atoms, f"xin{blk % 2}")
        xo = xtsr.next()
        for hf in range(2):
            pb = bank()
            for c4 in range(4):
                c = hf * 4 + c4
                transpose(PS.v(pb, (c4 * 128, c4 * 128 + 128)), xi.v((c * 128, c * 128 + 128)))
            copy("act" if hf == 0 else "dve", xo.v((hf * 4, hf * 4 + 4)), PS.custom(pb * 512, [[128, 4], [1, 128]]))
        dma("sp", xs[0][:, :, blk * 128:(blk + 1) * 128].rearrange("c p t -> p c t"), xo.v().ap,
            xo.v().atoms, [("xs", 0, blk)], f"xts{blk % 2}")

    ck(3)
    def wpanels(l):
        lst = []
        wv = win_bf[l].rearrange("(c p) n -> p c n", p=128)
        for j in range(11):
            lst.append(wv[:, :, j * 512:(j + 1) * 512])
        va = wa_bf[l].rearrange("(c p) n -> p c n", p=128)
        vb = wb_bf[l].rearrange("(c p) n -> p c n", p=128)
        for j in range(2):
            lst.append(va[:, :, j * 512:(j + 1) * 512])
            lst.append(vb[:, :, j * 512:(j + 1) * 512])
        vo = wo_bf[l].rearrange("(c p) n -> p c n", p=128)
        for j in range(2):
            lst.append(vo[:, :, j * 512:(j + 1) * 512])
        v3 = f1_bf[l].rearrange("(c p) n -> p c n", p=128)
        for j in range(8):
            lst.append(v3[:, :, j * 512:(j + 1) * 512])
        v4 = f2_bf[l].rearrange("(q c p) n -> q p c n", c=8, p=128)
        for hf in range(2):
            for q in range(4):
                lst.append(v4[q][:, :, hf * 512:(hf + 1) * 512])
        return lst

    plist = []
    for l in range(L):
        pl = wpanels(l)
        assert len(pl) == 33
        for t in range(NT):
            for pap in pl:
                plist.append((l, pap))
    wst = {"cur": 0, "nl": 0}

    def wget():
        while wst["nl"] < len(plist) and wst["nl"] <= wst["cur"] + PREF:
            i = wst["nl"]
            l, pap = plist[i]
            sl = wslots[i % NSLOT]
            dma("sp", sl.v().ap, pap, castkeys[l], sl.v().atoms, f"w{i % NSLOT}")
            wst["nl"] += 1
        sl = wslots[wst["cur"] % NSLOT]
        wst["cur"] += 1
        return sl

    def layer_setup(l):
        copy("dve", eh.v(p=(0, 33)), esk.v((l * 16, l * 16 + 16), p=(0, 33)))
        copy("dve", eh32.v(p=(0, 33)), eh.v(p=(0, 33)))
        tt("dve", el32.v(p=(0, 33)), esk.v((l * 16, l * 16 + 16), p=(0, 33)), eh32.v(p=(0, 33)), ALU.subtract)
        memset("dve", esx.v(p=(0, 33)), 0.0)
        for kh in range(NKV):
            copy("dve", esx.custom(kh * 512, [[256, 2], [128, 2], [1, 128]], p=(0, 1)),
                 eh.custom(kh * 4, [[1, 2], [2, 2], [0, 128]], p=(0, 1), span=4))
            copy("dve", esx.custom(kh * 512, [[256, 2], [128, 2], [1, 128]], p=(32, 33)),
                 el32.custom(kh * 4, [[1, 2], [2, 2], [0, 128]], p=(32, 33), span=4))
        dma("sp", gvbc.v().ap, a_gv[l:l + 1, :].partition_broadcast(128).rearrange("p a b -> p (a b)"),
            [], gvbc.v().atoms, "gv")
        dma("sp", wsraw.v().ap, a_ws[l].rearrange("g p q -> p g q"), [], wsraw.v().atoms, "ws")
        for hf in range(2):
            pb = bank()
            for g4 in range(4):
                g = hf * 4 + g4
                transpose(PS.v(pb, (g4 * 128, g4 * 128 + 128)), wsraw.v(g))
            copy("dve", WsT.v((hf * 4, hf * 4 + 4)), PS.custom(pb * 512, [[128, 4], [1, 128]]))
        memset("dve", bs32.v(p=(0, 33)), 0.0)
        dma("sp", bs32.v(p=(0, 1)).ap, a_bs[l:l + 1, :], [], bs32.v().atoms, "bs")
        dma("sp", bs32.v(p=(32, 33)).ap, a_bs[l:l + 1, :], [], bs32.v().atoms, "bs")
        copy("dve", bhi.v(p=(0, 33)), bs32.v(p=(0, 33)))
        copy("dve", bhi32.v(p=(0, 33)), bhi.v(p=(0, 33)))
        tt("dve", blo32.v(p=(0, 33)), bs32.v(p=(0, 33)), bhi32.v(p=(0, 33)), ALU.subtract)
        copy("dve", bsx.v(p=(0, 33)), bhi.v(p=(0, 33)))
        copy("dve", bsx.v(p=(32, 33)), blo32.v(p=(32, 33)))

    def tile_geom(t):
        ts = t % TPS
        hp = 128 if ts != 0 else 0
        hn = 128 if ts != TPS - 1 else 0
        return t * TT, hp, hn, TT + hp + hn

    def prologue(l, t):
        T0, hp, hn, NE = tile_geom(t)
        xb, hb = xT[t % 2], hT
        src = xs[l % 2][:, :, T0 - hp:T0 - hp + NE].rearrange("c p t -> p c t")
        rk = [("xs", l % 2, b) for b in range((T0 - hp) // 128, (T0 - hp + NE) // 128)]
        dma("sp", xb.v(None, (0, NE)).ap, src, rk, xb.v(None, (0, NE)).atoms, f"xld{t % 2}")
        act(sq.v(None, (0, NE)), xb.v(None, (0, NE)), AF.Square)
        segs = [(0, 512)] + ([(512, NE)] if NE > 512 else [])
        for (e0, e1) in segs:
            pb = bank()
            for c in range(KC):
                mm(PS.v(pb, (0, e1 - e0)), ones_bf.v(), sq.v(c, (e0, e1)), c == 0, c == KC - 1)
            rstd(rs.v((e0, e1)), PS.v(pb, (0, e1 - e0)), 1.0 / D)
        for c in range(KC):
            stt("dve", hb.v(c, (0, NE)), xb.v(c, (0, NE)),
                g1col.v((l * 8 + c, l * 8 + c + 1)), rs.v((0, NE)), ALU.mult, ALU.mult)

    slopes = [float(2.0 ** (-8.0 * (h + 1) / NH)) for h in range(NH)]

    def body(l, t, last):
        T0, hp, hn, NE = tile_geom(t)
        NEB = NE // 128
        xb, hb = xT[t % 2], hT
        cen = (hp, hp + TT)

        def fm_proj(wp, oc4, rhs_buf, rhs_cols):
            pb = bank()
            for kc in range(KC):
                mm(PS.v(pb), wp.v(kc, (oc4 * 128, oc4 * 128 + 128)), rhs_buf.v(kc, rhs_cols), kc == 0, kc == KC - 1)
            return pb

        for pi in range(6):
            wp = wget()
            dst, fn = ((sga, AF.Sigmoid), (sgb, AF.Sigmoid), (um, AF.Gelu))[pi // 2]
            for oc4 in range(4):
                oc = (pi % 2) * 4 + oc4
                pb = fm_proj(wp, oc4, hb, cen)
                act(dst.v(oc), PS.v(pb), fn)
        ck(6)
        wp6 = wget()
        wp7 = wget()
        for b in range(NBLK):
            g = gvb.next()
            c0 = hp + b * 128
            for hf, wp in enumerate((wp6, wp7)):
                pb = bank()
                for kc in range(KC):
                    mm(PS.v(pb), hb.v(kc, (c0, c0 + 128)), wp.v(kc), kc == 0, kc == KC - 1)
                act(g.v((hf * 512, hf * 512 + 512)), PS.v(pb), AF.Gelu)
            act(junk.v(), g.v(), AF.Square)
            sc.add("dve", lambda e, o=ssb.v((b, b + 1)).ap, i=junk.v().ap: e.reduce_sum(out=o, in_=i, axis=AX.X),
                   r=junk.v().atoms, w=ssb.v((b, b + 1)).atoms)
            rstd(ssb.v((4 + b, 5 + b)), ssb.v((b, b + 1)), 1.0 / 1024)
            stt("dve", vpp.v(b), g.v(), ssb.v((4 + b, 5 + b)), gvbc.v(), ALU.mult, ALU.mult)
        for b in range(NBLK):
            for gh in range(2):
                pb = bank()
                for gi in range(4):
                    g = gh * 4 + gi
                    out = PS.v(pb, (gi * 128, gi * 128 + 128))
                    mm(out, vpp.v(b, (g * 128, g * 128 + 128)), WsT.v(g), True, False)
                    mm(out, ones33.v(p=(0, 33)), bsx.v((g * 128, g * 128 + 128), p=(0, 33)), False, True)
                u_ = um.custom(gh * 4 * TT + b * 128, [[TT, 4], [1, 128]])
                tt("dve", u_, PS.custom(pb * 512, [[128, 4], [1, 128]]), u_, ALU.mult)
        ck(7)
        pend = []

        def qk_norm_start(pa, n, gcol, dst):
            s_ = bfr.next()
            act(s_.v((0, n)), PS.v(pa, (0, n)), AF.Square)
            pend.append((pa, n, gcol, dst, s_))

        def qk_norm_flush():
            while pend:
                pa, n, gcol, dst, s_ = pend.pop(0)
                pb2 = bank()
                mm(PS.v(pb2, (0, n)), bd_bf.v(), s_.v((0, n)), True, True)
                r_ = f32r.next()
                rstd(r_.v((0, n)), PS.v(pb2, (0, n)), 1.0 / HD)
                if isinstance(dst, tuple):
                    kh_, e0_, e1_ = dst
                    for hf_ in range(2):
                        pp_ = (hf_ * 64, hf_ * 64 + 64)
                        stt("dve", KT.v(kh_, hf_, (e0_, e1_), p=pp_), PS.v(pa, (0, n), p=pp_),
                            gcol.v((l, l + 1), p=pp_), r_.v((0, n), p=pp_), ALU.mult, ALU.mult)
                else:
                    stt("dve", dst, PS.v(pa, (0, n)), gcol, r_.v((0, n)), ALU.mult, ALU.mult)

        for pi in range(2):
            wp = wget()
            for oc4 in range(4):
                oc = pi * 4 + oc4
                pa = fm_proj(wp, oc4, hb, cen)
                qk_norm_flush()
                qk_norm_start(pa, TT, gqcol.v((l, l + 1)), QT.v(oc))
        wp = wget()
        segs = [(0, 512)] + ([(512, NE)] if NE > 512 else [])
        memset("pool", KT.custom(768, [[1536, 4], [1, 768]], p=(0, 64)), 0.0)
        memset("pool", KT.custom(0, [[1536, 4], [1, 768]], p=(64, 128)), 0.0)
        for kc in range(KC):
            copy("pool", wkd.custom(kc * 512, [[128, 4], [64, 2], [1, 64]]),
                 wp.custom(kc * 512, [[64, 4], [0, 2], [1, 64]], span=256))
        for kh in range(NKV):
            for (e0, e1) in segs:
                n = e1 - e0
                pa = bank()
                for kc in range(KC):
                    mm(PS.v(pa, (0, n)), wkd.v(kc, kh), hb.v(kc, (e0, e1)), kc == 0, kc == KC - 1)
                qk_norm_flush()
                qk_norm_start(pa, n, gkcol, (kh, e0, e1))
        for eb in range(NEB):
            pv = bank()
            for kc in range(KC):
                mm(PS.v(pv, (0, 256)), hb.v(kc, (eb * 128, eb * 128 + 128)), wp.v(kc, (256, 512)), kc == 0, kc == KC - 1)
            if eb == 0:
                qk_norm_flush()
            copy("act" if eb % 2 == 0 else "dve", VD.custom(eb * NKV * 128, [[128, 4], [64, 2], [1, 64]]),
                 PS.custom(pv * 512, [[64, 4], [0, 2], [1, 64]], span=256))
        qk_norm_flush()

        ck(8)
        units = [(b, kh) for b in range(NBLK) for kh in range(NKV)]
        stA = {}

        def stageA(u):
            b, kh = u
            qb = b + hp // 128
            pts = []
            for j in range(3):
                kb = qb + j - 1
                if kb < 0 or kb >= NEB:
                    continue
                ps_s = bank()
                for g in range(GQ):
                    hf, gg = g % 2, g // 2
                    c0 = hf * 256 + gg * 128
                    mm(PS.v(ps_s, (c0, c0 + 128)), KT.v(kh, hf, (kb * 128, kb * 128 + 128)),
                       QT.v(2 * kh + gg, (b * 128, b * 128 + 128)), True, True)
                ck(8.1)
                pt = bfr.next()
                act(pt.v(), PS.v(ps_s), AF.Exp, scale=HD ** -0.5)
                ck(8.2)
                tt("pool", pt.v(), pt.v(), MT.v(kh, j), ALU.mult)
                ck(8.3)
                pts.append((kb, pt))
            stA[u] = pts

        def stageB(u):
            b, kh = u
            pts = stA.pop(u)
            po, pd = bank(), bank()
            for i, (kb, pt) in enumerate(pts):
                mm(PS.v(po), VD.v(kb, kh), pt.v(), i == 0, i == len(pts) - 1)
                mm(PS.v(pd), ones_bf.v(), pt.v(), i == 0, False)
            mm(PS.v(pd), ones33.v(p=(0, 33)), esx.v(kh, p=(0, 33)), False, True)
            ck(8.4)
            rd = f32r.next()
            act(rd.v(), PS.v(pd), AF.Ln)
            act(rd.v(), rd.v(), AF.Exp, scale=-1.0)
            ck(8.5)
            for hf in range(2):
                pp = (hf * 64, hf * 64 + 64)
                o_ = OT.custom(2 * kh * TT + b * 128, [[TT, 2], [1, 128]], p=pp)
                tt("dve", o_, PS.custom(po * 512 + hf * 256, [[128, 2], [1, 128]], p=pp),
                   rd.custom(hf * 256, [[128, 2], [1, 128]], p=pp), ALU.mult)

        LAG = 2
        for i, u in enumerate(units):
            stageA(u)
            if i >= LAG:
                stageB(units[i - LAG])
        for u in units[-LAG:]:
            stageB(u)

        for hf in range(2):
            wpa = wget()
            wpb = wget()
            for oc4 in range(4):
                oc = hf * 4 + oc4
                pa = fm_proj(wpa, oc4, um, (0, TT))
                pb = fm_proj(wpb, oc4, OT, (0, TT))
                ta = f32r.next()
                tb = f32r.next()
                tt("dve", ta.v(), PS.v(pa), sga.v(oc), ALU.mult)
                tt("dve", tb.v(), PS.v(pb), sgb.v(oc), ALU.mult)
                tt("pool", mixT.v(oc), ta.v(), tb.v(), ALU.add)
        for hf in range(2):
            wp = wget()
            for oc4 in range(4):
                oc = hf * 4 + oc4
                pb = fm_proj(wp, oc4, mixT, (0, TT))
                tt("dve", xb.v(oc, cen), PS.v(pb), xb.v(oc, cen), ALU.add)
        ck(10)
        act(sq.v(None, (0, TT)), xb.v(None, cen), AF.Square)
        pb = bank()
        for c in range(KC):
            mm(PS.v(pb), ones_bf.v(), sq.v(c, (0, TT)), c == 0, c == KC - 1)
        rstd(rs.v((0, TT)), PS.v(pb), 1.0 / D)
        for c in range(KC):
            stt("dve", h2T.v(c), xb.v(c, cen),
                g2col.v((l * 8 + c, l * 8 + c + 1)), rs.v((0, TT)), ALU.mult, ALU.mult)
        for pi in range(8):
            wp = wget()
            for oc4 in range(4):
                fc = pi * 4 + oc4
                pb = fm_proj(wp, oc4, h2T, (0, TT))
                rl = f32r.next()
                act(rl.v(), PS.v(pb), AF.Relu)
                tt("pool", actT.v(fc), rl.v(), rl.v(), ALU.mult)
        ck(11)
        return (l, t, last)

    def body2(l, t, last):
        T0, hp, hn, NE = tile_geom(t)
        xb = xT[t % 2]
        cen = (hp, hp + TT)
        for hf in range(2):
            banks4 = [bank() for _ in range(4)]
            for q in range(4):
                wp = wget()
                for oc4 in range(4):
                    for kc in range(KC):
                        mm(PS.v(banks4[oc4]), wp.v(kc, (oc4 * 128, oc4 * 128 + 128)), actT.v(q * 8 + kc),
                           q == 0 and kc == 0, q == 3 and kc == KC - 1)
            for oc4 in range(4):
                oc = hf * 4 + oc4
                tt("dve", xb.v(oc, cen), PS.v(banks4[oc4]), xb.v(oc, cen), ALU.add)
        if not last:
            dst = xs[(l + 1) % 2][:, :, T0:T0 + TT].rearrange("c p t -> p c t")
            wk = [("xs", (l + 1) % 2, b) for b in range(T0 // 128, (T0 + TT) // 128)]
            dma("sp", dst, xb.v(None, cen).ap, xb.v(None, cen).atoms, wk, f"xst{t % 2}")
        else:
            for b in range(NBLK):
                k = xinr.i % 2
                xi = xinr.next()
                for hf in range(2):
                    pb = bank()
                    for c4 in range(4):
                        c = hf * 4 + c4
                        transpose(PS.v(pb, (c4 * 128, c4 * 128 + 128)), xb.v(c, (hp + b * 128, hp + b * 128 + 128)))
                    copy("act" if hf == 0 else "dve", xi.v((hf * 512, hf * 512 + 512)), PS.v(pb))
                r0 = T0 + b * 128
                dma("sp", y_out[r0:r0 + 128, :], xi.v().ap, xi.v().atoms, [("y", r0 // 128)], f"yst{k}")

    for l in range(L):
        layer_setup(l)
        ck(4)
        prologue(l, 0)
        ck(5)
        for t in range(NT):
            st = body(l, t, l == L - 1)
            if t + 1 < NT:
                prologue(l, t + 1)
            body2(*st)
    sc.add("sp", None, r=[("y", b) for b in range(NBT)])

    sc.finalize()
    dnames = sorted(sc.dma_count.keys())
    dsem = {n: es.enter_context(nc.semaphore("d_" + n)) for n in dnames}
    esem = {e: es.enter_context(nc.semaphore("e_" + e)) for e in ("pe", "act", "dve", "pool", "sp")}
    counts = {}
    with nc.Block() as block:
        @block.tensor
        def _(h):
            counts["pe"] = sc.emit_engine("pe", h, esem, dsem)

        @block.scalar
        def _(h):
            counts["act"] = sc.emit_engine("act", h, esem, dsem)

        @block.vector
        def _(h):
            counts["dve"] = sc.emit_engine("dve", h, esem, dsem)

        @block.gpsimd
        def _(h):
            counts["pool"] = sc.emit_engine("pool", h, esem, dsem)

        @block.sync
        def _(h):
            counts["sp"] = sc.emit_engine("sp", h, esem, dsem)
    es.close()
    kinfo = dict(counts=counts, nops=len(sc.ops), nwait=sc.nwait, sbuf_end=cur[0])
    return nc, kinfo


def _mult_table():
    s_ = np.arange(128)[:, None]
    q_ = np.arange(128)[None, :]
    slopes = np.exp2(-8.0 * (np.arange(NH, dtype=np.float64) + 1.0) / NH)
    out = np.zeros((128, NKV, 3, 2, 2, 128), np.float32)
    for j in range(3):
        dist = np.abs((s_ + (j - 1) * 128) - q_).astype(np.float64)
        valid = dist <= 128
        for kh in range(NKV):
            for hf in range(2):
                for gg in range(2):
                    h = kh * GQ + hf + 2 * gg
                    out[:, kh, j, hf, gg, :] = np.where(valid, np.exp(-slopes[h] * dist), 0.0)
    return out.reshape(128, -1)


def make_in_maps(inputs, ncores, nseq, L):
    x = np.asarray(inputs["x"], np.float32)
    B, S, _ = x.shape
    f = lambda k: np.ascontiguousarray(np.asarray(inputs[k], np.float32)[:L])
    g1 = f("ln1_g").reshape(L, 8, 128).transpose(2, 0, 1).reshape(128, L * 8)
    g2 = f("ln2_g").reshape(L, 8, 128).transpose(2, 0, 1).reshape(128, L * 8)
    gq = np.tile(f("b_q_norm_g").T, (2, 1))
    gk = np.tile(f("b_k_norm_g").T, (2, 1))
    shared = {
        "w_in": f("w_in"), "w_branch_a": f("w_branch_a"), "w_branch_b": f("w_branch_b"),
        "w_out": f("w_out"), "w_ff1": f("w_ff1"), "w_ff2": f("w_ff2"),
        "a_w_s": f("a_w_s"), "a_b_s": f("a_b_s").reshape(L, 1024), "a_norm_g": f("a_norm_g"),
        "b_sink": f("b_sink").reshape(1, L * 16),
        "g1col": np.ascontiguousarray(g1), "g2col": np.ascontiguousarray(g2),
        "gqcol": np.ascontiguousarray(gq), "gkcol": np.ascontiguousarray(gk),
        "c_ident": np.eye(128, dtype=np.float32), "c_mtab": _mult_table(),
    }
    maps = []
    for c in range(ncores):
        m = dict(shared)
        m["x"] = np.ascontiguousarray(x[c * nseq:(c + 1) * nseq].reshape(nseq * S, D))
        maps.append(m)
    return maps


def kernel(**inputs):
    x = np.asarray(inputs["x"])
    B, S, _ = x.shape
    L = np.asarray(inputs["w_in"]).shape[0]
    ncores = 8
    nseq = B // ncores
    nc, _ = build_program(nseq, S, L)
    maps = make_in_maps(inputs, ncores, nseq, L)
    res = run_bass_kernel_spmd(nc, maps, core_ids=list(range(ncores)))
    outs = [np.asarray(r["y"]).reshape(nseq, S, D) for r in res.results]
    return np.concatenate(outs, axis=0).astype(np.float32)
```

```python
import contextlib
import numpy as np
import concourse.bass as bass
import concourse.mybir as mybir
from concourse.ap import AP
from concourse.bass_utils import run_bass_kernel_spmd

F32 = mybir.dt.float32
BF16 = mybir.dt.bfloat16
AF = mybir.ActivationFunctionType
ALU = mybir.AluOpType
AX = mybir.AxisListType

D = 1024
KC = 8
TT = 512
NBLK = 4
HD = 64
NH = 16
NKV = 4
GQ = 4
DFF = 4096
INW = 5632
EPS = 1e-6
ATOM = 512
NSLOT = 4
PREF = 2


class Sched:
    def __init__(self):
        self.ops = []
        self.lastw = {}
        self.lastr = {}
        self.dma_count = {}
        self.stopped = False

    def add(self, eng, fn, r=(), w=(), dma=None, bulk=False):
        if self.stopped:
            return -1
        i = len(self.ops)
        deps = set()
        lastw, lastr = self.lastw, self.lastr
        for a in r:
            j = lastw.get(a)
            if j is not None:
                deps.add(j)
        for a in w:
            j = lastw.get(a)
            if j is not None:
                deps.add(j)
            rd = lastr.get(a)
            if rd:
                deps.update(rd.values())
        ekey = ("d", dma) if dma else eng
        for a in w:
            lastw[a] = i
            lastr[a] = {}
        for a in r:
            d = lastr.get(a)
            if d is None:
                d = lastr[a] = {}
            d[ekey] = i
        deps.discard(i)
        o = 0
        if dma:
            o = self.dma_count[dma] = self.dma_count.get(dma, 0) + 1
        self.ops.append([eng, fn, deps, dma, o, bulk, False, 0, None, None])
        return i

    def finalize(self):
        ops = self.ops
        for op in ops:
            eng = op[0]
            nd = set()
            for j in op[2]:
                y = ops[j]
                if y[3] is None and y[0] == eng and eng in ("pe", "sp"):
                    continue
                nd.add(j)
                if y[3] is None:
                    y[6] = True
            op[2] = nd
        cnt = {}
        for op in ops:
            if op[3] is None and op[6]:
                cnt[op[0]] = cnt.get(op[0], 0) + 1
                op[7] = cnt[op[0]]
        vc = {}
        nwait = 0
        for op in ops:
            e = op[0]
            my = vc.setdefault(e, {})
            waits = {}
            for j in op[2]:
                y = ops[j]
                if y[3] is not None:
                    key = ("d", y[3])
                    val = 16 * (self.dma_count[y[3]] if y[5] else y[4])
                else:
                    key = ("e", y[0])
                    val = y[7]
                if my.get(key, 0) >= val:
                    continue
                if waits.get(key, 0) < val:
                    waits[key] = val
            for key, val in waits.items():
                my[key] = val
            for j in op[2]:
                s = ops[j][9]
                if s:
                    for k, v in s.items():
                        if my.get(k, 0) < v:
                            my[k] = v
            op[8] = list(waits.items())
            nwait += len(op[8])
            if op[3] is not None:
                snap = dict(my)
                if not op[5]:
                    snap[("d", op[3])] = max(snap.get(("d", op[3]), 0), 16 * op[4])
                op[9] = snap
            elif op[6]:
                snap = dict(my)
                snap[("e", e)] = op[7]
                op[9] = snap
        self.nwait = nwait

    def emit_engine(self, eng, h, esem, dsem):
        def semof(key):
            return dsem[key[1]] if key[0] == "d" else esem[key[1]]

        n = 0
        for op in self.ops:
            if op[0] != eng:
                continue
            waits = op[8]
            fn = op[1]
            if fn is None or op[3] is not None:
                for key, val in waits:
                    h.wait_ge(semof(key), val)
                    n += 1
                if fn is None:
                    continue
                ins = fn(h)
            else:
                for key, val in waits[1:]:
                    h.wait_ge(semof(key), val)
                    n += 1
                ins = fn(h)
                if waits:
                    ins._wait_ge(semof(waits[0][0]), waits[0][1])
            n += 1
            if op[3] is not None:
                ins.then_inc(dsem[op[3]], 16)
            elif op[6]:
                ins.then_inc(esem[eng], 1)
        return n


class Acc:
    __slots__ = ("ap", "atoms")

    def __init__(self, ap, atoms):
        self.ap = ap
        self.atoms = atoms


class Buf:
    def __init__(self, name, t, fs, esz, track=None, base=0, pstep=None, eoff=0, atom=ATOM):
        self.name = name
        self.t = t
        self.fs = tuple(fs)
        self.esz = esz
        st = []
        s = 1
        for n in reversed(self.fs):
            st.append(s)
            s *= n
        self.st = tuple(reversed(st))
        self.pstep = pstep or s
        self.eoff = eoff
        self.track = track or name
        self.base = base
        self.cache = {}
        self.atom = atom

    def v(self, *idx, p=(0, 128)):
        key = (idx, p)
        c = self.cache.get(key)
        if c is not None:
            return c
        assert len(idx) <= len(self.fs)
        idx = tuple(idx) + (None,) * (len(self.fs) - len(idx))
        off = p[0] * self.pstep + self.eoff
        dims = [[self.pstep, p[1] - p[0]]]
        rng = []
        for i, ix in enumerate(idx):
            if ix is None:
                lo, hi = 0, self.fs[i]
            elif isinstance(ix, tuple):
                lo, hi = ix
            else:
                off += ix * self.st[i]
                rng.append((self.st[i], ix, ix + 1))
                continue
            assert 0 <= lo < hi <= self.fs[i], (self.name, idx)
            off += lo * self.st[i]
            dims.append([self.st[i], hi - lo])
            rng.append((self.st[i], lo, hi))
        ap = AP(self.t, off, dims)
        atoms = self._atoms(rng)
        c = Acc(ap, atoms)
        self.cache[key] = c
        return c

    def _atoms(self, rng):
        runs = [(0,)]
        starts = [0]
        for (st, lo, hi) in rng[:-1]:
            starts = [s + st * k for s in starts for k in range(lo, hi)]
        st, lo, hi = rng[-1]
        atoms = set()
        for s in starts:
            b0 = self.base + (s + st * lo) * self.esz
            b1 = self.base + (s + st * (hi - 1) + 1) * self.esz
            for a in range(b0 // self.atom, (b1 - 1) // self.atom + 1):
                atoms.add((self.track, a))
        return frozenset(atoms)

    def custom(self, off_elems, dims, p=(0, 128), span=None):
        off = p[0] * self.pstep + self.eoff + off_elems
        ap = AP(self.t, off, [[self.pstep, p[1] - p[0]]] + [list(d) for d in dims])
        if span is None:
            span = 1 + sum(s * (n - 1) for s, n in dims)
        b0 = self.base + off_elems * self.esz
        b1 = self.base + (off_elems + span) * self.esz
        atoms = frozenset((self.track, a) for a in range(b0 // self.atom, (b1 - 1) // self.atom + 1))
        return Acc(ap, atoms)


class Ring:
    def __init__(self, bufs):
        self.bufs = bufs
        self.i = 0

    def next(self):
        b = self.bufs[self.i % len(self.bufs)]
        self.i += 1
        return b


SB_LO = 16512
SB_HI = 229344


def build_program(nseq, S, L):
    NTOK = nseq * S
    NT = NTOK // TT
    TPS = S // TT
    NBT = NTOK // 128
    nc = bass.Bass("TRN2", target_bir_lowering=False)
    sc = Sched()
    import os as _os
    kstop = float(_os.environ.get("KSTOP", "99"))

    def ck(n):
        if n >= kstop:
            sc.stopped = True
    es = contextlib.ExitStack()

    def dram(name, shape, dt, kind):
        return nc.dram_tensor(name, shape, dt, kind=kind).ap()

    x_in = dram("x", [NTOK, D], F32, "ExternalInput")
    y_out = dram("y", [NTOK, D], F32, "ExternalOutput")
    w_in = dram("w_in", [L, D, INW], F32, "ExternalInput")
    w_a = dram("w_branch_a", [L, D, D], F32, "ExternalInput")
    w_b = dram("w_branch_b", [L, D, D], F32, "ExternalInput")
    w_o = dram("w_out", [L, D, D], F32, "ExternalInput")
    w_f1 = dram("w_ff1", [L, D, DFF], F32, "ExternalInput")
    w_f2 = dram("w_ff2", [L, DFF, D], F32, "ExternalInput")
    a_ws = dram("a_w_s", [L, 8, 128, 128], F32, "ExternalInput")
    a_bs = dram("a_b_s", [L, 1024], F32, "ExternalInput")
    a_gv = dram("a_norm_g", [L, 1024], F32, "ExternalInput")
    sink = dram("b_sink", [1, L * 16], F32, "ExternalInput")
    g1c = dram("g1col", [128, L * 8], F32, "ExternalInput")
    g2c = dram("g2col", [128, L * 8], F32, "ExternalInput")
    gqc = dram("gqcol", [128, L], F32, "ExternalInput")
    gkc = dram("gkcol", [128, L], F32, "ExternalInput")
    c_ident = dram("c_ident", [128, 128], F32, "ExternalInput")
    c_mtab = dram("c_mtab", [128, NKV * 3 * 512], F32, "ExternalInput")
    xs = [dram(f"xs{i}", [KC, 128, NTOK], F32, "Internal") for i in range(2)]
    win_bf = dram("win_bf", [L, D, INW], BF16, "Internal")
    wa_bf = dram("wa_bf", [L, D, D], BF16, "Internal")
    wb_bf = dram("wb_bf", [L, D, D], BF16, "Internal")
    wo_bf = dram("wo_bf", [L, D, D], BF16, "Internal")
    f1_bf = dram("f1_bf", [L, D, DFF], BF16, "Internal")
    f2_bf = dram("f2_bf", [L, DFF, D], BF16, "Internal")

    cur = [SB_LO]

    def sb(name, fs, dt, at=None):
        esz = 2 if dt == BF16 else 4
        n = 1
        for k in fs:
            n *= k
        nbytes = n * esz
        if at is None:
            at = cur[0]
            cur[0] = (at + nbytes + 31) // 32 * 32
            assert cur[0] <= SB_HI, ("SBUF overflow", name, cur[0])
        t = nc.alloc_sbuf_tensor_at(name, [128] + list(fs), dt, offset=at)
        return Buf(name, t, fs, esz, track="SB", base=at)

    xT = [sb(f"xT{i}", [KC, 768], F32) for i in range(2)]
    hT = sb("hT", [KC, 768], BF16)
    sq = sb("sq", [KC, 768], BF16)
    rs = sb("rs", [768], F32)
    sga = sb("sga", [KC, TT], BF16)
    sgb = sb("sgb", [KC, TT], BF16)
    um = sb("um", [KC, TT], BF16)
    OT = sb("OT", [KC, TT], BF16, at=sq.base)
    h2T = Buf("h2T", sga.t, [KC, TT], 2, track="SB", base=sga.base)
    wkd = sb("wkd", [KC, NKV, 128], BF16, at=sq.base)
    A0 = cur[0]
    cur[0] += 32768
    actT = sb("actT", [32, TT], BF16, at=A0)
    QT = sb("QT", [KC, TT], BF16, at=A0)
    KT = sb("KT", [NKV, 2, 768], BF16, at=A0 + 8192)
    VD = sb("VD", [6, NKV, 128], BF16, at=A0 + 20480)
    gvbP = [[sb(f"gvb0_{p}", [1024], F32, at=xT[1 - p].base), sb(f"gvb1_{p}", [1024], F32, at=xT[1 - p].base + 4096)]
            for p in range(2)]
    vppP = [sb(f"vpp_{p}", [NBLK, 1024], BF16, at=xT[1 - p].base + 8192) for p in range(2)]
    junk = sb("junk", [1024], F32, at=sq.base + 8192)
    mixT = sb("mixT", [KC, TT], BF16, at=A0)
    wsraw = sb("wsraw", [8, 128], F32, at=A0 + 20480)
    bs32 = sb("bs32", [1024], F32, at=A0)
    bhi32 = sb("bhi32", [1024], F32, at=A0 + 4096)
    blo32 = sb("blo32", [1024], F32, at=A0 + 8192)
    bhi = sb("bhi", [1024], BF16, at=A0 + 12288)
    xinr = Ring([sb("xin0", [1024], F32, at=A0), sb("xin1", [1024], F32, at=A0 + 4096)])
    xtsr = Ring([sb("xts0", [KC, 128], F32, at=A0 + 8192), sb("xts1", [KC, 128], F32, at=A0 + 12288)])
    f32r = Ring([sb(f"f32r{i}", [TT], F32) for i in range(4)])
    bfr = Ring([sb(f"bfr{i}", [TT], BF16) for i in range(10)])
    wslots = [sb(f"wsl{i}", [KC, 512], BF16) for i in range(NSLOT)]
    MT = sb("MT", [NKV, 3, 512], BF16)
    esx = sb("esx", [NKV, 512], BF16)
    eh = sb("eh", [16], BF16)
    eh32 = sb("eh32", [16], F32)
    el32 = sb("el32", [16], F32)
    gvbc = sb("gvbc", [1024], F32)
    WsT = sb("WsT", [8, 128], BF16)
    bsx = sb("bsx", [1024], BF16)
    ident = sb("ident", [128], F32)
    ones_bf = sb("ones_bf", [128], BF16)
    bd_bf = sb("bd_bf", [128], BF16)
    ones33 = sb("ones33", [128], BF16)
    g1col = sb("g1col_s", [L * 8], F32)
    g2col = sb("g2col_s", [L * 8], F32)
    gqcol = sb("gqcol_s", [L], F32)
    gkcol = sb("gkcol_s", [L], F32)
    eskraw = sb("eskraw", [L * 16], F32)
    esk = sb("esk", [L * 16], F32)
    ssb = sb("ssb", [8], F32)
    eps_col = sb("eps_col", [1], F32)

    pst = es.enter_context(nc.psum_tensor("ps", [128, 8, 512], F32))
    PS = Buf("ps", pst, [8, 512], 4, track="PS", atom=2048)
    bank_ctr = [0]

    def bank():
        b = bank_ctr[0] % 8
        bank_ctr[0] += 1
        return b

    def mm(out, lhsT, rhs, start, stop):
        sc.add("pe", lambda e, o=out.ap, l=lhsT.ap, r=rhs.ap, s0=start, s1=stop:
               e.matmul(o, lhsT=l, rhs=r, start=s0, stop=s1),
               r=lhsT.atoms | rhs.atoms, w=out.atoms)

    def act(out, in_, func, scale=1.0, bias=None):
        if bias is None:
            sc.add("act", lambda e, o=out.ap, i=in_.ap, f=func, s=scale:
                   e.activation(out=o, in_=i, func=f, scale=s),
                   r=in_.atoms, w=out.atoms)
        else:
            sc.add("act", lambda e, o=out.ap, i=in_.ap, f=func, s=scale, b=bias.ap:
                   e.activation(out=o, in_=i, func=f, scale=s, bias=b),
                   r=in_.atoms | bias.atoms, w=out.atoms)

    def tt(eng, out, in0, in1, op):
        sc.add(eng, lambda e, o=out.ap, a=in0.ap, b=in1.ap, op=op:
               e.tensor_tensor(out=o, in0=a, in1=b, op=op),
               r=in0.atoms | in1.atoms, w=out.atoms)

    def stt(eng, out, in0, scalar, in1, op0, op1):
        if isinstance(scalar, Acc):
            sc.add(eng, lambda e, o=out.ap, a=in0.ap, s=scalar.ap, b=in1.ap, op0=op0, op1=op1:
                   e.scalar_tensor_tensor(out=o, in0=a, scalar=s, in1=b, op0=op0, op1=op1),
                   r=in0.atoms | in1.atoms | scalar.atoms, w=out.atoms)
        else:
            sc.add(eng, lambda e, o=out.ap, a=in0.ap, s=float(scalar), b=in1.ap, op0=op0, op1=op1:
                   e.scalar_tensor_tensor(out=o, in0=a, scalar=s, in1=b, op0=op0, op1=op1),
                   r=in0.atoms | in1.atoms, w=out.atoms)

    def recip(eng, out, in_):
        sc.add(eng, lambda e, o=out.ap, i=in_.ap: e.reciprocal(out=o, in_=i),
               r=in_.atoms, w=out.atoms)

    def copy(eng, out, in_):
        if eng == "act":
            sc.add("act", lambda e, o=out.ap, i=in_.ap: e.copy(out=o, in_=i), r=in_.atoms, w=out.atoms)
        else:
            sc.add(eng, lambda e, o=out.ap, i=in_.ap: e.tensor_copy(out=o, in_=i), r=in_.atoms, w=out.atoms)

    def memset(eng, out, val):
        sc.add(eng, lambda e, o=out.ap, v=val: e.memset(o, v), w=out.atoms)

    def dma(eng, out_ap, in_ap, r, w, sem, bulk=False):
        sc.add(eng, lambda e, o=out_ap, i=in_ap: e.dma_start(out=o, in_=i), r=r, w=w, dma=sem, bulk=bulk)

    def transpose(out, in_):
        sc.add("pe", lambda e, o=out.ap, i=in_.ap, idn=ident.v().ap: e.transpose(o, i, idn),
               r=in_.atoms | ident.v().atoms, w=out.atoms)

    def rstd(out, in_, scale):
        act(out, in_, AF.Ln, scale=scale, bias=eps_col.v())
        act(out, out, AF.Exp, scale=-0.5)

    memset("dve", eps_col.v(), EPS)
    memset("dve", ones_bf.v(), 1.0)
    memset("dve", bd_bf.v(), 0.0)
    memset("dve", bd_bf.v((0, 64), p=(0, 64)), 1.0)
    memset("dve", bd_bf.v((64, 128), p=(64, 128)), 1.0)
    memset("dve", ones33.v(), 0.0)
    memset("dve", ones33.v(p=(0, 1)), 1.0)
    memset("dve", ones33.v(p=(32, 33)), 1.0)
    dma("sp", ident.v().ap, c_ident, [], ident.v().atoms, "c0")
    dma("sp", g1col.v().ap, g1c, [], g1col.v().atoms, "c2")
    dma("sp", g2col.v().ap, g2c, [], g2col.v().atoms, "c3")
    dma("sp", gqcol.v().ap, gqc, [], gqcol.v().atoms, "c4")
    dma("sp", gkcol.v().ap, gkc, [], gkcol.v().atoms, "c5")
    dma("sp", eskraw.v().ap, sink.partition_broadcast(128).rearrange("p a b -> p (a b)"), [], eskraw.v().atoms, "c6")
    act(esk.v(), eskraw.v(), AF.Exp)
    ck(1)

    for blk in range(NBT):
        xi = xinr.next()
        dma("sp", xi.v().ap, x_in[blk * 128:(blk + 1) * 128, :], [], xi.v().atoms, f"xin{blk % 2}")
        xo = xtsr.next()
        for hf in range(2):
            pb = bank()
            for c4 in range(4):
                c = hf * 4 + c4
                transpose(PS.v(pb, (c4 * 128, c4 * 128 + 128)), xi.v((c * 128, c * 128 + 128)))
            copy("act" if hf == 0 else "dve", xo.v((hf * 4, hf * 4 + 4)), PS.custom(pb * 512, [[128, 4], [1, 128]]))
        dma("sp", xs[0][:, :, blk * 128:(blk + 1) * 128].rearrange("c p t -> p c t"), xo.v().ap,
            xo.v().atoms, [("xs", 0, blk)], f"xts{blk % 2}")

    ck(2)
    NCS = 6
    CW = 2048
    cst_in = [sb(f"cst_in{i}", [CW], F32, at=xT[0].base + i * 12288) for i in range(NCS)]
    cst_out = [sb(f"cst_out{i}", [CW], BF16, at=xT[0].base + i * 12288 + 8192) for i in range(NCS)]
    assert xT[0].base + NCS * 12288 <= sq.base + 12288
    for kh in range(NKV):
        a = cst_in[kh % NCS]
        dma("sp", a.v((0, 1536)).ap, c_mtab[:, kh * 1536:(kh + 1) * 1536], [], a.v((0, 1536)).atoms, f"cin{kh % NCS}")
        copy("dve", MT.custom(kh * 1536, [[1, 1536]]), a.v((0, 1536)))
    castkeys = {}
    chunks = []
    for l in range(L):
        castkeys[l] = []
        for (dst, src, rows, cols) in ((win_bf[l], w_in[l], D, INW), (wa_bf[l], w_a[l], D, D), (wb_bf[l], w_b[l], D, D),
                                       (wo_bf[l], w_o[l], D, D), (f1_bf[l], w_f1[l], D, DFF), (f2_bf[l], w_f2[l], DFF, D)):
            for r0 in range(0, rows, 128):
                for c0 in range(0, cols, CW):
                    cw = min(CW, cols - c0)
                    k = ("wbf", l, len(castkeys[l]))
                    castkeys[l].append(k)
                    chunks.append((dst[r0:r0 + 128, c0:c0 + cw], src[r0:r0 + 128, c0:c0 + cw], cw, k))
    AHEAD = 3

    def cast_store(n):
        dst_ap, src_ap, cw, k = chunks[n]
        b = cst_out[(n + NKV) % NCS]
        dma("sp", dst_ap, b.v((0, cw)).ap, b.v((0, cw)).atoms, [k], f"cout{(n + NKV) % NCS}")

    for n, (dst_ap, src_ap, cw, k) in enumerate(chunks):
        i = (n + NKV) % NCS
        a, b = cst_in[i], cst_out[i]
        dma("sp", a.v((0, cw)).ap, src_ap, [], a.v((0, cw)).atoms, f"cin{i}")
        copy(("dve", "act")[n % 2], b.v((0, cw)), a.v((0, cw)))
        if n >= AHEAD:
            cast_store(n - AHEAD)
    for n in range(max(0, len(chunks) - AHEAD), len(chunks)):
        cast_store(n)

    ck(3)
    def wpanels(l):
        lst = []
        wv = win_bf[l].rearrange("(c p) n -> p c n", p=128)
        for j in (8, 9, 10, 0, 1, 2, 3, 4, 5, 6, 7):
            lst.append(wv[:, :, j * 512:(j + 1) * 512])
        va = wa_bf[l].rearrange("(c p) n -> p c n", p=128)
        vb = wb_bf[l].rearrange("(c p) n -> p c n", p=128)
        for j in range(2):
            lst.append(va[:, :, j * 512:(j + 1) * 512])
            lst.append(vb[:, :, j * 512:(j + 1) * 512])
        vo = wo_bf[l].rearrange("(c p) n -> p c n", p=128)
        for j in range(2):
            lst.append(vo[:, :, j * 512:(j + 1) * 512])
        v3 = f1_bf[l].rearrange("(c p) n -> p c n", p=128)
        for j in range(8):
            lst.append(v3[:, :, j * 512:(j + 1) * 512])
        v4 = f2_bf[l].rearrange("(q c p) n -> q p c n", c=8, p=128)
        for hf in range(2):
            for q in range(4):
                lst.append(v4[q][:, :, hf * 512:(hf + 1) * 512])
        return lst

    plist = []
    for l in range(L):
        pl = wpanels(l)
        assert len(pl) == 33
        for t in range(NT):
            for pap in pl:
                plist.append((l, pap))
    wst = {"cur": 0, "nl": 0}

    def wget():
        while wst["nl"] < len(plist) and wst["nl"] <= wst["cur"] + PREF:
            i = wst["nl"]
            l, pap = plist[i]
            sl = wslots[i % NSLOT]
            dma("sp", sl.v().ap, pap, castkeys[l], sl.v().atoms, f"w{i % NSLOT}")
            wst["nl"] += 1
        sl = wslots[wst["cur"] % NSLOT]
        wst["cur"] += 1
        return sl

    def layer_setup(l):
        copy("dve", eh.v(p=(0, 33)), esk.v((l * 16, l * 16 + 16), p=(0, 33)))
        copy("dve", eh32.v(p=(0, 33)), eh.v(p=(0, 33)))
        tt("dve", el32.v(p=(0, 33)), esk.v((l * 16, l * 16 + 16), p=(0, 33)), eh32.v(p=(0, 33)), ALU.subtract)
        memset("dve", esx.v(p=(0, 33)), 0.0)
        for kh in range(NKV):
            copy("dve", esx.custom(kh * 512, [[256, 2], [128, 2], [1, 128]], p=(0, 1)),
                 eh.custom(kh * 4, [[1, 2], [2, 2], [0, 128]], p=(0, 1), span=4))
            copy("dve", esx.custom(kh * 512, [[256, 2], [128, 2], [1, 128]], p=(32, 33)),
                 el32.custom(kh * 4, [[1, 2], [2, 2], [0, 128]], p=(32, 33), span=4))
        dma("sp", gvbc.v().ap, a_gv[l:l + 1, :].partition_broadcast(128).rearrange("p a b -> p (a b)"),
            [], gvbc.v().atoms, "gv")
        dma("sp", wsraw.v().ap, a_ws[l].rearrange("g p q -> p g q"), [], wsraw.v().atoms, "ws")
        for hf in range(2):
            pb = bank()
            for g4 in range(4):
                g = hf * 4 + g4
                transpose(PS.v(pb, (g4 * 128, g4 * 128 + 128)), wsraw.v(g))
            copy("dve", WsT.v((hf * 4, hf * 4 + 4)), PS.custom(pb * 512, [[128, 4], [1, 128]]))
        memset("dve", bs32.v(p=(0, 33)), 0.0)
        dma("sp", bs32.v(p=(0, 1)).ap, a_bs[l:l + 1, :], [], bs32.v().atoms, "bs")
        dma("sp", bs32.v(p=(32, 33)).ap, a_bs[l:l + 1, :], [], bs32.v().atoms, "bs")
        copy("dve", bhi.v(p=(0, 33)), bs32.v(p=(0, 33)))
        copy("dve", bhi32.v(p=(0, 33)), bhi.v(p=(0, 33)))
        tt("dve", blo32.v(p=(0, 33)), bs32.v(p=(0, 33)), bhi32.v(p=(0, 33)), ALU.subtract)
        copy("dve", bsx.v(p=(0, 33)), bhi.v(p=(0, 33)))
        copy("dve", bsx.v(p=(32, 33)), blo32.v(p=(32, 33)))

    def tile_geom(t):
        ts = t % TPS
        hp = 128 if ts != 0 else 0
        hn = 128 if ts != TPS - 1 else 0
        return t * TT, hp, hn, TT + hp + hn

    def prologue(l, t, part=None):
        T0, hp, hn, NE = tile_geom(t)
        xb, hb = xT[t % 2], hT
        if part in (None, 0):
            src = xs[l % 2][:, :, T0 - hp:T0 - hp + NE].rearrange("c p t -> p c t")
            rk = [("xs", l % 2, b) for b in range((T0 - hp) // 128, (T0 - hp + NE) // 128)]
            dma("sp", xb.v(None, (0, NE)).ap, src, rk, xb.v(None, (0, NE)).atoms, f"xld{t % 2}")
        if part in (None, 1):
            for c0 in range(0, KC, 4):
                act(sq.v((c0, c0 + 4), (0, NE)), xb.v((c0, c0 + 4), (0, NE)), AF.Square)
        if part in (None, 2):
            segs = [(0, 512)] + ([(512, NE)] if NE > 512 else [])
            for (e0, e1) in segs:
                pb = bank()
                for c in range(KC):
                    mm(PS.v(pb, (0, e1 - e0)), ones_bf.v(), sq.v(c, (e0, e1)), c == 0, c == KC - 1)
                rstd(rs.v((e0, e1)), PS.v(pb, (0, e1 - e0)), 1.0 / D)
            for c in range(KC):
                stt("dve", hb.v(c, (0, NE)), xb.v(c, (0, NE)),
                    g1col.v((l * 8 + c, l * 8 + c + 1)), rs.v((0, NE)), ALU.mult, ALU.mult)

    slopes = [float(2.0 ** (-8.0 * (h + 1) / NH)) for h in range(NH)]

    def body(l, t, last, nxt):
        T0, hp, hn, NE = tile_geom(t)
        NEB = NE // 128
        xb, hb = xT[t % 2], hT
        cen = (hp, hp + TT)
        gvb = Ring(gvbP[t % 2])
        vpp = vppP[t % 2]

        def fm_proj(wp, oc4, rhs_buf, rhs_cols):
            pb = bank()
            for kc in range(KC):
                mm(PS.v(pb), wp.v(kc, (oc4 * 128, oc4 * 128 + 128)), rhs_buf.v(kc, rhs_cols), kc == 0, kc == KC - 1)
            return pb

        ck(6)
        pend = []

        def qk_norm_start(pa, n, gcol, dst):
            s_ = bfr.next()
            act(s_.v((0, n)), PS.v(pa, (0, n)), AF.Square)
            pend.append((pa, n, gcol, dst, s_))

        def qk_norm_flush():
            while pend:
                pa, n, gcol, dst, s_ = pend.pop(0)
                pb2 = bank()
                mm(PS.v(pb2, (0, n)), bd_bf.v(), s_.v((0, n)), True, True)
                r_ = f32r.next()
                rstd(r_.v((0, n)), PS.v(pb2, (0, n)), 1.0 / HD)
                if isinstance(dst, tuple):
                    kh_, e0_, e1_ = dst
                    for hf_ in range(2):
                        pp_ = (hf_ * 64, hf_ * 64 + 64)
                        stt("dve", KT.v(kh_, hf_, (e0_, e1_), p=pp_), PS.v(pa, (0, n), p=pp_),
                            gcol.v((l, l + 1), p=pp_), r_.v((0, n), p=pp_), ALU.mult, ALU.mult)
                else:
                    stt("dve", dst, PS.v(pa, (0, n)), gcol, r_.v((0, n)), ALU.mult, ALU.mult)

        for pi in range(2):
            wp = wget()
            for oc4 in range(4):
                oc = pi * 4 + oc4
                pa = fm_proj(wp, oc4, hb, cen)
                qk_norm_flush()
                qk_norm_start(pa, TT, gqcol.v((l, l + 1)), QT.v(oc))
        wp = wget()
        segs = [(0, 512)] + ([(512, NE)] if NE > 512 else [])
        memset("pool", KT.custom(768, [[1536, 4], [1, 768]], p=(0, 64)), 0.0)
        memset("pool", KT.custom(0, [[1536, 4], [1, 768]], p=(64, 128)), 0.0)
        for kc in range(KC):
            copy("pool", wkd.custom(kc * 512, [[128, 4], [64, 2], [1, 64]]),
                 wp.custom(kc * 512, [[64, 4], [0, 2], [1, 64]], span=256))
        for kh in range(NKV):
            for (e0, e1) in segs:
                n = e1 - e0
                pa = bank()
                for kc in range(KC):
                    mm(PS.v(pa, (0, n)), wkd.v(kc, kh), hb.v(kc, (e0, e1)), kc == 0, kc == KC - 1)
                qk_norm_flush()
                qk_norm_start(pa, n, gkcol, (kh, e0, e1))
        for eb in range(NEB):
            pv = bank()
            for kc in range(KC):
                mm(PS.v(pv, (0, 256)), hb.v(kc, (eb * 128, eb * 128 + 128)), wp.v(kc, (256, 512)), kc == 0, kc == KC - 1)
            if eb == 0:
                qk_norm_flush()
            copy("act" if eb % 2 == 0 else "dve", VD.custom(eb * NKV * 128, [[128, 4], [64, 2], [1, 64]]),
                 PS.custom(pv * 512, [[64, 4], [0, 2], [1, 64]], span=256))
        qk_norm_flush()

        items = []
        wst_ = {}

        def mk_fm_item(pi, oc4):
            def item():
                if oc4 == 0:
                    wst_[pi] = wget()
                wp = wst_[pi]
                dst, fn = ((sga, AF.Sigmoid), (sgb, AF.Sigmoid), (um, AF.Gelu))[pi // 2]
                oc = (pi % 2) * 4 + oc4
                pb = fm_proj(wp, oc4, hb, cen)
                act(dst.v(oc), PS.v(pb), fn)
            return item

        for pi in range(6):
            for oc4 in range(4):
                items.append(mk_fm_item(pi, oc4))

        def mk_gv_item(b):
            def item():
                if b == 0:
                    wst_["g6"] = wget()
                    wst_["g7"] = wget()
                g = gvb.next()
                c0 = hp + b * 128
                for hf, wp in enumerate((wst_["g6"], wst_["g7"])):
                    pb = bank()
                    for kc in range(KC):
                        mm(PS.v(pb), hb.v(kc, (c0, c0 + 128)), wp.v(kc), kc == 0, kc == KC - 1)
                    act(g.v((hf * 512, hf * 512 + 512)), PS.v(pb), AF.Gelu)
                act(junk.v(), g.v(), AF.Square)
                sc.add("dve", lambda e, o=ssb.v((b, b + 1)).ap, i=junk.v().ap: e.reduce_sum(out=o, in_=i, axis=AX.X),
                       r=junk.v().atoms, w=ssb.v((b, b + 1)).atoms)
                rstd(ssb.v((4 + b, 5 + b)), ssb.v((b, b + 1)), 1.0 / 1024)
                stt("dve", vpp.v(b), g.v(), ssb.v((4 + b, 5 + b)), gvbc.v(), ALU.mult, ALU.mult)
            return item

        for b in range(NBLK):
            items.append(mk_gv_item(b))

        def mk_sp_item(b, gh):
            def item():
                pb = bank()
                for gi in range(4):
                    g = gh * 4 + gi
                    out = PS.v(pb, (gi * 128, gi * 128 + 128))
                    mm(out, vpp.v(b, (g * 128, g * 128 + 128)), WsT.v(g), True, False)
                    mm(out, ones33.v(p=(0, 33)), bsx.v((g * 128, g * 128 + 128), p=(0, 33)), False, True)
                u_ = um.custom(gh * 4 * TT + b * 128, [[TT, 4], [1, 128]])
                tt("dve", u_, PS.custom(pb * 512, [[128, 4], [1, 128]]), u_, ALU.mult)
            return item

        for b in range(NBLK):
            for gh in range(2):
                items.append(mk_sp_item(b, gh))
        ck(7)
        ck(8)
        units = [(b, kh) for b in range(NBLK) for kh in range(NKV)]
        stA = {}

        def stageA(u):
            b, kh = u
            qb = b + hp // 128
            pts = []
            for j in range(3):
                kb = qb + j - 1
                if kb < 0 or kb >= NEB:
                    continue
                ps_s = bank()
                for g in range(GQ):
                    hf, gg = g % 2, g // 2
                    c0 = hf * 256 + gg * 128
                    mm(PS.v(ps_s, (c0, c0 + 128)), KT.v(kh, hf, (kb * 128, kb * 128 + 128)),
                       QT.v(2 * kh + gg, (b * 128, b * 128 + 128)), True, True)
                ck(8.1)
                pt = bfr.next()
                act(pt.v(), PS.v(ps_s), AF.Exp, scale=HD ** -0.5)
                ck(8.2)
                tt("pool", pt.v(), pt.v(), MT.v(kh, j), ALU.mult)
                ck(8.3)
                pts.append((kb, pt))
            stA[u] = pts

        def stageB(u):
            b, kh = u
            pts = stA.pop(u)
            po, pd = bank(), bank()
            for i, (kb, pt) in enumerate(pts):
                mm(PS.v(po), VD.v(kb, kh), pt.v(), i == 0, i == len(pts) - 1)
                mm(PS.v(pd), ones_bf.v(), pt.v(), i == 0, False)
            mm(PS.v(pd), ones33.v(p=(0, 33)), esx.v(kh, p=(0, 33)), False, True)
            ck(8.4)
            rd = f32r.next()
            act(rd.v(), PS.v(pd), AF.Ln)
            act(rd.v(), rd.v(), AF.Exp, scale=-1.0)
            ck(8.5)
            for hf in range(2):
                pp = (hf * 64, hf * 64 + 64)
                o_ = OT.custom(2 * kh * TT + b * 128, [[TT, 2], [1, 128]], p=pp)
                tt("dve", o_, PS.custom(po * 512 + hf * 256, [[128, 2], [1, 128]], p=pp),
                   rd.custom(hf * 256, [[128, 2], [1, 128]], p=pp), ALU.mult)

        LAG = 2
        nit = len(items)
        done = 0
        PRE = 6
        while done < PRE:
            items[done]()
            done += 1
        for i, u in enumerate(units):
            stageA(u)
            if i >= LAG:
                stageB(units[i - LAG])
            tgt = PRE + (i + 1) * (nit - PRE) // len(units)
            while done < tgt:
                items[done]()
                done += 1
        for u in units[-LAG:]:
            stageB(u)
        while done < nit:
            items[done]()
            done += 1
        ck(9)
        if nxt is not None:
            prologue(nxt[0], nxt[1], part=0)
        for hf in range(2):
            wpa = wget()
            wpb = wget()
            for oc4 in range(4):
                oc = hf * 4 + oc4
                pa = fm_proj(wpa, oc4, um, (0, TT))
                pb = fm_proj(wpb, oc4, OT, (0, TT))
                ta = f32r.next()
                tb = f32r.next()
                tt("dve", ta.v(), PS.v(pa), sga.v(oc), ALU.mult)
                tt("dve", tb.v(), PS.v(pb), sgb.v(oc), ALU.mult)
                tt("pool", mixT.v(oc), ta.v(), tb.v(), ALU.add)
        for hf in range(2):
            wp = wget()
            for oc4 in range(4):
                oc = hf * 4 + oc4
                pb = fm_proj(wp, oc4, mixT, (0, TT))
                tt("dve", xb.v(oc, cen), PS.v(pb), xb.v(oc, cen), ALU.add)
        ck(10)
        act(sq.v(None, (0, TT)), xb.v(None, cen), AF.Square)
        pb = bank()
        for c in range(KC):
            mm(PS.v(pb), ones_bf.v(), sq.v(c, (0, TT)), c == 0, c == KC - 1)
        rstd(rs.v((0, TT)), PS.v(pb), 1.0 / D)
        if nxt is not None:
            prologue(nxt[0], nxt[1], part=1)
        for c in range(KC):
            stt("dve", h2T.v(c), xb.v(c, cen),
                g2col.v((l * 8 + c, l * 8 + c + 1)), rs.v((0, TT)), ALU.mult, ALU.mult)
        for pi in range(8):
            if pi == 3 and nxt is not None:
                prologue(nxt[0], nxt[1], part=2)
            wp = wget()
            for oc4 in range(4):
                fc = pi * 4 + oc4
                pb = fm_proj(wp, oc4, h2T, (0, TT))
                rl = f32r.next()
                act(rl.v(), PS.v(pb), AF.Relu)
                tt("pool", actT.v(fc), rl.v(), rl.v(), ALU.mult)
        ck(11)
        return (l, t, last)

    def body2(l, t, last):
        T0, hp, hn, NE = tile_geom(t)
        xb = xT[t % 2]
        cen = (hp, hp + TT)
        for hf in range(2):
            banks4 = [bank() for _ in range(4)]
            for q in range(4):
                wp = wget()
                for oc4 in range(4):
                    for kc in range(KC):
                        mm(PS.v(banks4[oc4]), wp.v(kc, (oc4 * 128, oc4 * 128 + 128)), actT.v(q * 8 + kc),
                           q == 0 and kc == 0, q == 3 and kc == KC - 1)
            for oc4 in range(4):
                oc = hf * 4 + oc4
                tt("dve", xb.v(oc, cen), PS.v(banks4[oc4]), xb.v(oc, cen), ALU.add)
        if not last:
            dst = xs[(l + 1) % 2][:, :, T0:T0 + TT].rearrange("c p t -> p c t")
            wk = [("xs", (l + 1) % 2, b) for b in range(T0 // 128, (T0 + TT) // 128)]
            dma("sp", dst, xb.v(None, cen).ap, xb.v(None, cen).atoms, wk, f"xst{t % 2}")
        else:
            for b in range(NBLK):
                k = xinr.i % 2
                xi = xinr.next()
                for hf in range(2):
                    pb = bank()
                    for c4 in range(4):
                        c = hf * 4 + c4
                        transpose(PS.v(pb, (c4 * 128, c4 * 128 + 128)), xb.v(c, (hp + b * 128, hp + b * 128 + 128)))
                    copy("act" if hf == 0 else "dve", xi.v((hf * 512, hf * 512 + 512)), PS.v(pb))
                r0 = T0 + b * 128
                dma("sp", y_out[r0:r0 + 128, :], xi.v().ap, xi.v().atoms, [("y", r0 // 128)], f"yst{k}")

    for l in range(L):
        layer_setup(l)
        ck(4)
        prologue(l, 0)
        ck(5)
        for t in range(NT):
            nxt = (l, t + 1) if t + 1 < NT else None
            st = body(l, t, l == L - 1, nxt)
            body2(*st)
    sc.add("sp", None, r=[("y", b) for b in range(NBT)])

    sc.finalize()
    dnames = sorted(sc.dma_count.keys())
    dsem = {n: es.enter_context(nc.semaphore("d_" + n)) for n in dnames}
    esem = {e: es.enter_context(nc.semaphore("e_" + e)) for e in ("pe", "act", "dve", "pool", "sp")}
    counts = {}
    with nc.Block() as block:
        @block.tensor
        def _(h):
            counts["pe"] = sc.emit_engine("pe", h, esem, dsem)

        @block.scalar
        def _(h):
            counts["act"] = sc.emit_engine("act", h, esem, dsem)

        @block.vector
        def _(h):
            counts["dve"] = sc.emit_engine("dve", h, esem, dsem)

        @block.gpsimd
        def _(h):
            counts["pool"] = sc.emit_engine("pool", h, esem, dsem)

        @block.sync
        def _(h):
            counts["sp"] = sc.emit_engine("sp", h, esem, dsem)
    es.close()
    kinfo = dict(counts=counts, nops=len(sc.ops), nwait=sc.nwait, sbuf_end=cur[0])
    return nc, kinfo


def _mult_table():
    s_ = np.arange(128)[:, None]
    q_ = np.arange(128)[None, :]
    slopes = np.exp2(-8.0 * (np.arange(NH, dtype=np.float64) + 1.0) / NH)
    out = np.zeros((128, NKV, 3, 2, 2, 128), np.float32)
    for j in range(3):
        dist = np.abs((s_ + (j - 1) * 128) - q_).astype(np.float64)
        valid = dist <= 128
        for kh in range(NKV):
            for hf in range(2):
                for gg in range(2):
                    h = kh * GQ + hf + 2 * gg
                    out[:, kh, j, hf, gg, :] = np.where(valid, np.exp(-slopes[h] * dist), 0.0)
    return out.reshape(128, -1)


def make_in_maps(inputs, ncores, nseq, L):
    x = np.asarray(inputs["x"], np.float32)
    B, S, _ = x.shape
    f = lambda k: np.ascontiguousarray(np.asarray(inputs[k], np.float32)[:L])
    g1 = f("ln1_g").reshape(L, 8, 128).transpose(2, 0, 1).reshape(128, L * 8)
    g2 = f("ln2_g").reshape(L, 8, 128).transpose(2, 0, 1).reshape(128, L * 8)
    gq = np.tile(f("b_q_norm_g").T, (2, 1))
    gk = np.tile(f("b_k_norm_g").T, (2, 1))
    shared = {
        "w_in": f("w_in"), "w_branch_a": f("w_branch_a"), "w_branch_b": f("w_branch_b"),
        "w_out": f("w_out"), "w_ff1": f("w_ff1"), "w_ff2": f("w_ff2"),
        "a_w_s": f("a_w_s"), "a_b_s": f("a_b_s").reshape(L, 1024), "a_norm_g": f("a_norm_g"),
        "b_sink": f("b_sink").reshape(1, L * 16),
        "g1col": np.ascontiguousarray(g1), "g2col": np.ascontiguousarray(g2),
        "gqcol": np.ascontiguousarray(gq), "gkcol": np.ascontiguousarray(gk),
        "c_ident": np.eye(128, dtype=np.float32), "c_mtab": _mult_table(),
    }
    maps = []
    for c in range(ncores):
        m = dict(shared)
        m["x"] = np.ascontiguousarray(x[c * nseq:(c + 1) * nseq].reshape(nseq * S, D))
        maps.append(m)
    return maps


def kernel(**inputs):
    x = np.asarray(inputs["x"])
    B, S, _ = x.shape
    L = np.asarray(inputs["w_in"]).shape[0]
    ncores = 8
    nseq = B // ncores
    nc, _ = build_program(nseq, S, L)
    maps = make_in_maps(inputs, ncores, nseq, L)
    res = run_bass_kernel_spmd(nc, maps, core_ids=list(range(ncores)))
    outs = [np.asarray(r["y"]).reshape(nseq, S, D) for r in res.results]
    return np.concatenate(outs, axis=0).astype(np.float32)
```

```python
import contextlib
import numpy as np
import concourse.bass as bass
import concourse.mybir as mybir
from concourse.ap import AP
from concourse.bass_utils import run_bass_kernel_spmd

F32 = mybir.dt.float32
BF16 = mybir.dt.bfloat16
AF = mybir.ActivationFunctionType
ALU = mybir.AluOpType
AX = mybir.AxisListType

D = 1024
KC = 8
TT = 512
NBLK = 4
HD = 64
NH = 16
NKV = 4
GQ = 4
DFF = 4096
INW = 5632
EPS = 1e-6
ATOM = 512
NSLOT = 4
PREF = 2


class Sched:
    def __init__(self):
        self.ops = []
        self.lastw = {}
        self.lastr = {}
        self.dma_count = {}
        self.stopped = False

    def add(self, eng, fn, r=(), w=(), dma=None, bulk=False):
        if self.stopped:
            return -1
        i = len(self.ops)
        deps = set()
        lastw, lastr = self.lastw, self.lastr
        for a in r:
            j = lastw.get(a)
            if j is not None:
                deps.add(j)
        for a in w:
            j = lastw.get(a)
            if j is not None:
                deps.add(j)
            rd = lastr.get(a)
            if rd:
                deps.update(rd.values())
        ekey = ("d", dma) if dma else eng
        for a in w:
            lastw[a] = i
            lastr[a] = {}
        for a in r:
            d = lastr.get(a)
            if d is None:
                d = lastr[a] = {}
            d[ekey] = i
        deps.discard(i)
        o = 0
        if dma:
            o = self.dma_count[dma] = self.dma_count.get(dma, 0) + 1
        self.ops.append([eng, fn, deps, dma, o, bulk, False, 0, None, None])
        return i

    def finalize(self):
        ops = self.ops
        for op in ops:
            eng = op[0]
            nd = set()
            for j in op[2]:
                y = ops[j]
                if y[3] is None and y[0] == eng and eng in ("pe", "sp"):
                    continue
                nd.add(j)
                if y[3] is None:
                    y[6] = True
            op[2] = nd
        cnt = {}
        for op in ops:
            if op[3] is None and op[6]:
                cnt[op[0]] = cnt.get(op[0], 0) + 1
                op[7] = cnt[op[0]]
        vc = {}
        nwait = 0
        for op in ops:
            e = op[0]
            my = vc.setdefault(e, {})
            waits = {}
            for j in op[2]:
                y = ops[j]
                if y[3] is not None:
                    key = ("d", y[3])
                    val = 16 * (self.dma_count[y[3]] if y[5] else y[4])
                else:
                    key = ("e", y[0])
                    val = y[7]
                if my.get(key, 0) >= val:
                    continue
                if waits.get(key, 0) < val:
                    waits[key] = val
            for key, val in waits.items():
                my[key] = val
            for j in op[2]:
                s = ops[j][9]
                if s:
                    for k, v in s.items():
                        if my.get(k, 0) < v:
                            my[k] = v
            op[8] = list(waits.items())
            nwait += len(op[8])
            if op[3] is not None:
                snap = dict(my)
                if not op[5]:
                    snap[("d", op[3])] = max(snap.get(("d", op[3]), 0), 16 * op[4])
                op[9] = snap
            elif op[6]:
                snap = dict(my)
                snap[("e", e)] = op[7]
                op[9] = snap
        self.nwait = nwait

    def emit_engine(self, eng, h, esem, dsem):
        def semof(key):
            return dsem[key[1]] if key[0] == "d" else esem[key[1]]

        n = 0
        for op in self.ops:
            if op[0] != eng:
                continue
            waits = op[8]
            fn = op[1]
            if fn is None or op[3] is not None:
                for key, val in waits:
                    h.wait_ge(semof(key), val)
                    n += 1
                if fn is None:
                    continue
                ins = fn(h)
            else:
                for key, val in waits[1:]:
                    h.wait_ge(semof(key), val)
                    n += 1
                ins = fn(h)
                if waits:
                    ins._wait_ge(semof(waits[0][0]), waits[0][1])
            n += 1
            if op[3] is not None:
                ins.then_inc(dsem[op[3]], 16)
            elif op[6]:
                ins.then_inc(esem[eng], 1)
        return n


class Acc:
    __slots__ = ("ap", "atoms")

    def __init__(self, ap, atoms):
        self.ap = ap
        self.atoms = atoms


class Buf:
    def __init__(self, name, t, fs, esz, track=None, base=0, pstep=None, eoff=0, atom=ATOM):
        self.name = name
        self.t = t
        self.fs = tuple(fs)
        self.esz = esz
        st = []
        s = 1
        for n in reversed(self.fs):
            st.append(s)
            s *= n
        self.st = tuple(reversed(st))
        self.pstep = pstep or s
        self.eoff = eoff
        self.track = track or name
        self.base = base
        self.cache = {}
        self.atom = atom

    def v(self, *idx, p=(0, 128)):
        key = (idx, p)
        c = self.cache.get(key)
        if c is not None:
            return c
        assert len(idx) <= len(self.fs)
        idx = tuple(idx) + (None,) * (len(self.fs) - len(idx))
        off = p[0] * self.pstep + self.eoff
        dims = [[self.pstep, p[1] - p[0]]]
        rng = []
        for i, ix in enumerate(idx):
            if ix is None:
                lo, hi = 0, self.fs[i]
            elif isinstance(ix, tuple):
                lo, hi = ix
            else:
                off += ix * self.st[i]
                rng.append((self.st[i], ix, ix + 1))
                continue
            assert 0 <= lo < hi <= self.fs[i], (self.name, idx)
            off += lo * self.st[i]
            dims.append([self.st[i], hi - lo])
            rng.append((self.st[i], lo, hi))
        ap = AP(self.t, off, dims)
        atoms = self._atoms(rng)
        c = Acc(ap, atoms)
        self.cache[key] = c
        return c

    def _atoms(self, rng):
        runs = [(0,)]
        starts = [0]
        for (st, lo, hi) in rng[:-1]:
            starts = [s + st * k for s in starts for k in range(lo, hi)]
        st, lo, hi = rng[-1]
        atoms = set()
        for s in starts:
            b0 = self.base + (s + st * lo) * self.esz
            b1 = self.base + (s + st * (hi - 1) + 1) * self.esz
            for a in range(b0 // self.atom, (b1 - 1) // self.atom + 1):
                atoms.add((self.track, a))
        return frozenset(atoms)

    def custom(self, off_elems, dims, p=(0, 128), span=None):
        off = p[0] * self.pstep + self.eoff + off_elems
        ap = AP(self.t, off, [[self.pstep, p[1] - p[0]]] + [list(d) for d in dims])
        if span is None:
            span = 1 + sum(s * (n - 1) for s, n in dims)
        b0 = self.base + off_elems * self.esz
        b1 = self.base + (off_elems + span) * self.esz
        atoms = frozenset((self.track, a) for a in range(b0 // self.atom, (b1 - 1) // self.atom + 1))
        return Acc(ap, atoms)


class Ring:
    def __init__(self, bufs):
        self.bufs = bufs
        self.i = 0

    def next(self):
        b = self.bufs[self.i % len(self.bufs)]
        self.i += 1
        return b


SB_LO = 16512
SB_HI = 229344


def build_program(nseq, S, L):
    NTOK = nseq * S
    NT = NTOK // TT
    TPS = S // TT
    NBT = NTOK // 128
    nc = bass.Bass("TRN2", target_bir_lowering=False)
    sc = Sched()
    import os as _os
    kstop = float(_os.environ.get("KSTOP", "99"))

    def ck(n):
        if n >= kstop:
            sc.stopped = True
    es = contextlib.ExitStack()

    def dram(name, shape, dt, kind):
        return nc.dram_tensor(name, shape, dt, kind=kind).ap()

    x_in = dram("x", [NTOK, D], F32, "ExternalInput")
    y_out = dram("y", [NTOK, D], F32, "ExternalOutput")
    w_in = dram("w_in", [L, D, INW], F32, "ExternalInput")
    w_a = dram("w_branch_a", [L, D, D], F32, "ExternalInput")
    w_b = dram("w_branch_b", [L, D, D], F32, "ExternalInput")
    w_o = dram("w_out", [L, D, D], F32, "ExternalInput")
    w_f1 = dram("w_ff1", [L, D, DFF], F32, "ExternalInput")
    w_f2 = dram("w_ff2", [L, DFF, D], F32, "ExternalInput")
    a_ws = dram("a_w_s", [L, 8, 128, 128], F32, "ExternalInput")
    a_bs = dram("a_b_s", [L, 1024], F32, "ExternalInput")
    a_gv = dram("a_norm_g", [L, 1024], F32, "ExternalInput")
    sink = dram("b_sink", [1, L * 16], F32, "ExternalInput")
    g1c = dram("g1col", [128, L * 8], F32, "ExternalInput")
    g2c = dram("g2col", [128, L * 8], F32, "ExternalInput")
    gqc = dram("gqcol", [128, L], F32, "ExternalInput")
    gkc = dram("gkcol", [128, L], F32, "ExternalInput")
    c_ident = dram("c_ident", [128, 128], F32, "ExternalInput")
    c_mtab = dram("c_mtab", [128, NKV * 3 * 512], F32, "ExternalInput")
    xs = [dram(f"xs{i}", [KC, 128, NTOK], F32, "Internal") for i in range(2)]
    win_bf = dram("win_bf", [L, D, INW], BF16, "Internal")
    wa_bf = dram("wa_bf", [L, D, D], BF16, "Internal")
    wb_bf = dram("wb_bf", [L, D, D], BF16, "Internal")
    wo_bf = dram("wo_bf", [L, D, D], BF16, "Internal")
    f1_bf = dram("f1_bf", [L, D, DFF], BF16, "Internal")
    f2_bf = dram("f2_bf", [L, DFF, D], BF16, "Internal")

    cur = [SB_LO]

    def sb(name, fs, dt, at=None):
        esz = 2 if dt == BF16 else 4
        n = 1
        for k in fs:
            n *= k
        nbytes = n * esz
        if at is None:
            at = cur[0]
            cur[0] = (at + nbytes + 31) // 32 * 32
            assert cur[0] <= SB_HI, ("SBUF overflow", name, cur[0])
        t = nc.alloc_sbuf_tensor_at(name, [128] + list(fs), dt, offset=at)
        return Buf(name, t, fs, esz, track="SB", base=at)

    xT = [sb(f"xT{i}", [KC, 768], F32) for i in range(2)]
    hT = sb("hT", [KC, 768], BF16)
    sq = sb("sq", [KC, 768], BF16)
    rs = sb("rs", [768], F32)
    sga = sb("sga", [KC, TT], BF16)
    sgb = sb("sgb", [KC, TT], BF16)
    um = sb("um", [KC, TT], BF16)
    OT = sb("OT", [KC, TT], BF16, at=sq.base)
    h2T = Buf("h2T", sga.t, [KC, TT], 2, track="SB", base=sga.base)
    wkd = sb("wkd", [KC, NKV, 128], BF16, at=sq.base)
    A0 = cur[0]
    cur[0] += 32768
    actT = sb("actT", [32, TT], BF16, at=A0)
    QT = sb("QT", [KC, TT], BF16, at=A0)
    KT = sb("KT", [NKV, 2, 768], BF16, at=A0 + 8192)
    VD = sb("VD", [6, NKV, 128], BF16, at=A0 + 20480)
    gvbP = [[sb(f"gvb0_{p}", [1024], F32, at=xT[1 - p].base), sb(f"gvb1_{p}", [1024], F32, at=xT[1 - p].base + 4096)]
            for p in range(2)]
    vppP = [sb(f"vpp_{p}", [NBLK, 1024], BF16, at=xT[1 - p].base + 8192) for p in range(2)]
    junk = sb("junk", [1024], F32, at=sq.base + 8192)
    mixT = sb("mixT", [KC, TT], BF16, at=A0)
    wsraw = sb("wsraw", [8, 128], F32, at=A0 + 20480)
    bs32 = sb("bs32", [1024], F32, at=A0)
    bhi32 = sb("bhi32", [1024], F32, at=A0 + 4096)
    blo32 = sb("blo32", [1024], F32, at=A0 + 8192)
    bhi = sb("bhi", [1024], BF16, at=A0 + 12288)
    xinr = Ring([sb("xin0", [1024], F32, at=A0), sb("xin1", [1024], F32, at=A0 + 4096)])
    xtsr = Ring([sb("xts0", [KC, 128], F32, at=A0 + 8192), sb("xts1", [KC, 128], F32, at=A0 + 12288)])
    f32r = Ring([sb(f"f32r{i}", [TT], F32) for i in range(4)])
    bfr = Ring([sb(f"bfr{i}", [TT], BF16) for i in range(10)])
    wslots = [sb(f"wsl{i}", [KC, 512], BF16) for i in range(NSLOT)]
    MT = sb("MT", [NKV, 3, 512], BF16)
    esx = sb("esx", [NKV, 512], BF16)
    eh = sb("eh", [16], BF16)
    eh32 = sb("eh32", [16], F32)
    el32 = sb("el32", [16], F32)
    gvbc = sb("gvbc", [1024], F32)
    WsT = sb("WsT", [8, 128], BF16)
    bsx = sb("bsx", [1024], BF16)
    ident = sb("ident", [128], F32)
    ones_bf = sb("ones_bf", [128], BF16)
    bd_bf = sb("bd_bf", [128], BF16)
    ones33 = sb("ones33", [128], BF16)
    g1col = sb("g1col_s", [L * 8], F32)
    g2col = sb("g2col_s", [L * 8], F32)
    gqcol = sb("gqcol_s", [L], F32)
    gkcol = sb("gkcol_s", [L], F32)
    eskraw = sb("eskraw", [L * 16], F32)
    esk = sb("esk", [L * 16], F32)
    ssb = sb("ssb", [8], F32)
    eps_col = sb("eps_col", [1], F32)

    pst = es.enter_context(nc.psum_tensor("ps", [128, 8, 512], F32))
    PS = Buf("ps", pst, [8, 512], 4, track="PS", atom=2048)
    bank_ctr = [0]

    def bank():
        b = bank_ctr[0] % 8
        bank_ctr[0] += 1
        return b

    def mm(out, lhsT, rhs, start, stop):
        sc.add("pe", lambda e, o=out.ap, l=lhsT.ap, r=rhs.ap, s0=start, s1=stop:
               e.matmul(o, lhsT=l, rhs=r, start=s0, stop=s1),
               r=lhsT.atoms | rhs.atoms, w=out.atoms)

    def act(out, in_, func, scale=1.0, bias=None):
        if bias is None:
            sc.add("act", lambda e, o=out.ap, i=in_.ap, f=func, s=scale:
                   e.activation(out=o, in_=i, func=f, scale=s),
                   r=in_.atoms, w=out.atoms)
        else:
            sc.add("act", lambda e, o=out.ap, i=in_.ap, f=func, s=scale, b=bias.ap:
                   e.activation(out=o, in_=i, func=f, scale=s, bias=b),
                   r=in_.atoms | bias.atoms, w=out.atoms)

    def tt(eng, out, in0, in1, op):
        sc.add(eng, lambda e, o=out.ap, a=in0.ap, b=in1.ap, op=op:
               e.tensor_tensor(out=o, in0=a, in1=b, op=op),
               r=in0.atoms | in1.atoms, w=out.atoms)

    def stt(eng, out, in0, scalar, in1, op0, op1):
        if isinstance(scalar, Acc):
            sc.add(eng, lambda e, o=out.ap, a=in0.ap, s=scalar.ap, b=in1.ap, op0=op0, op1=op1:
                   e.scalar_tensor_tensor(out=o, in0=a, scalar=s, in1=b, op0=op0, op1=op1),
                   r=in0.atoms | in1.atoms | scalar.atoms, w=out.atoms)
        else:
            sc.add(eng, lambda e, o=out.ap, a=in0.ap, s=float(scalar), b=in1.ap, op0=op0, op1=op1:
                   e.scalar_tensor_tensor(out=o, in0=a, scalar=s, in1=b, op0=op0, op1=op1),
                   r=in0.atoms | in1.atoms, w=out.atoms)

    def recip(eng, out, in_):
        sc.add(eng, lambda e, o=out.ap, i=in_.ap: e.reciprocal(out=o, in_=i),
               r=in_.atoms, w=out.atoms)

    def copy(eng, out, in_):
        if eng == "act":
            sc.add("act", lambda e, o=out.ap, i=in_.ap: e.copy(out=o, in_=i), r=in_.atoms, w=out.atoms)
        else:
            sc.add(eng, lambda e, o=out.ap, i=in_.ap: e.tensor_copy(out=o, in_=i), r=in_.atoms, w=out.atoms)

    def memset(eng, out, val):
        sc.add(eng, lambda e, o=out.ap, v=val: e.memset(o, v), w=out.atoms)

    def dma(eng, out_ap, in_ap, r, w, sem, bulk=False):
        sc.add(eng, lambda e, o=out_ap, i=in_ap: e.dma_start(out=o, in_=i), r=r, w=w, dma=sem, bulk=bulk)

    def transpose(out, in_):
        sc.add("pe", lambda e, o=out.ap, i=in_.ap, idn=ident.v().ap: e.transpose(o, i, idn),
               r=in_.atoms | ident.v().atoms, w=out.atoms)

    def rstd(out, in_, scale):
        act(out, in_, AF.Ln, scale=scale, bias=eps_col.v())
        act(out, out, AF.Exp, scale=-0.5)

    memset("dve", eps_col.v(), EPS)
    memset("dve", ones_bf.v(), 1.0)
    memset("dve", bd_bf.v(), 0.0)
    memset("dve", bd_bf.v((0, 64), p=(0, 64)), 1.0)
    memset("dve", bd_bf.v((64, 128), p=(64, 128)), 1.0)
    memset("dve", ones33.v(), 0.0)
    memset("dve", ones33.v(p=(0, 1)), 1.0)
    memset("dve", ones33.v(p=(32, 33)), 1.0)
    dma("sp", ident.v().ap, c_ident, [], ident.v().atoms, "c0")
    dma("sp", g1col.v().ap, g1c, [], g1col.v().atoms, "c2")
    dma("sp", g2col.v().ap, g2c, [], g2col.v().atoms, "c3")
    dma("sp", gqcol.v().ap, gqc, [], gqcol.v().atoms, "c4")
    dma("sp", gkcol.v().ap, gkc, [], gkcol.v().atoms, "c5")
    dma("sp", eskraw.v().ap, sink.partition_broadcast(128).rearrange("p a b -> p (a b)"), [], eskraw.v().atoms, "c6")
    act(esk.v(), eskraw.v(), AF.Exp)
    ck(1)

    for blk in range(NBT):
        xi = xinr.next()
        dma("sp", xi.v().ap, x_in[blk * 128:(blk + 1) * 128, :], [], xi.v().atoms, f"xin{blk % 2}")
        xo = xtsr.next()
        for hf in range(2):
            pb = bank()
            for c4 in range(4):
                c = hf * 4 + c4
                transpose(PS.v(pb, (c4 * 128, c4 * 128 + 128)), xi.v((c * 128, c * 128 + 128)))
            copy("act" if hf == 0 else "dve", xo.v((hf * 4, hf * 4 + 4)), PS.custom(pb * 512, [[128, 4], [1, 128]]))
        dma("sp", xs[0][:, :, blk * 128:(blk + 1) * 128].rearrange("c p t -> p c t"), xo.v().ap,
            xo.v().atoms, [("xs", 0, blk)], f"xts{blk % 2}")

    ck(2)
    NCS = 6
    CW = 2048
    cst_in = [sb(f"cst_in{i}", [CW], F32, at=xT[0].base + i * 12288) for i in range(NCS)]
    cst_out = [sb(f"cst_out{i}", [CW], BF16, at=xT[0].base + i * 12288 + 8192) for i in range(NCS)]
    assert xT[0].base + NCS * 12288 <= sq.base + 12288
    for kh in range(NKV):
        a = cst_in[kh % NCS]
        dma("sp", a.v((0, 1536)).ap, c_mtab[:, kh * 1536:(kh + 1) * 1536], [], a.v((0, 1536)).atoms, f"cin{kh % NCS}")
        copy("dve", MT.custom(kh * 1536, [[1, 1536]]), a.v((0, 1536)))
    castkeys = {}
    chunks = []
    for l in range(L):
        castkeys[l] = []
        for (dst, src, rows, cols) in ((win_bf[l], w_in[l], D, INW), (wa_bf[l], w_a[l], D, D), (wb_bf[l], w_b[l], D, D),
                                       (wo_bf[l], w_o[l], D, D), (f1_bf[l], w_f1[l], D, DFF), (f2_bf[l], w_f2[l], DFF, D)):
            for r0 in range(0, rows, 128):
                for c0 in range(0, cols, CW):
                    cw = min(CW, cols - c0)
                    k = ("wbf", l, len(castkeys[l]))
                    castkeys[l].append(k)
                    chunks.append((dst[r0:r0 + 128, c0:c0 + cw], src[r0:r0 + 128, c0:c0 + cw], cw, k))
    AHEAD = 3

    def cast_store(n):
        dst_ap, src_ap, cw, k = chunks[n]
        b = cst_out[(n + NKV) % NCS]
        dma("act", dst_ap, b.v((0, cw)).ap, b.v((0, cw)).atoms, [k], f"cout{(n + NKV) % NCS}")

    for n, (dst_ap, src_ap, cw, k) in enumerate(chunks):
        i = (n + NKV) % NCS
        a, b = cst_in[i], cst_out[i]
        dma("sp", a.v((0, cw)).ap, src_ap, [], a.v((0, cw)).atoms, f"cin{i}")
        copy(("dve", "act")[n % 2], b.v((0, cw)), a.v((0, cw)))
        if n >= AHEAD:
            cast_store(n - AHEAD)
    for n in range(max(0, len(chunks) - AHEAD), len(chunks)):
        cast_store(n)

    ck(3)
    def wpanels(l):
        lst = []
        wv = win_bf[l].rearrange("(c p) n -> p c n", p=128)
        for j in (8, 9, 10, 0, 1, 2, 3, 4, 5, 6, 7):
            lst.append(wv[:, :, j * 512:(j + 1) * 512])
        va = wa_bf[l].rearrange("(c p) n -> p c n", p=128)
        vb = wb_bf[l].rearrange("(c p) n -> p c n", p=128)
        for j in range(2):
            lst.append(va[:, :, j * 512:(j + 1) * 512])
            lst.append(vb[:, :, j * 512:(j + 1) * 512])
        vo = wo_bf[l].rearrange("(c p) n -> p c n", p=128)
        for j in range(2):
            lst.append(vo[:, :, j * 512:(j + 1) * 512])
        v3 = f1_bf[l].rearrange("(c p) n -> p c n", p=128)
        for j in range(8):
            lst.append(v3[:, :, j * 512:(j + 1) * 512])
        v4 = f2_bf[l].rearrange("(q c p) n -> q p c n", c=8, p=128)
        for hf in range(2):
            for q in range(4):
                lst.append(v4[q][:, :, hf * 512:(hf + 1) * 512])
        return lst

    plist = []
    for l in range(L):
        pl = wpanels(l)
        assert len(pl) == 33
        for t in range(NT):
            for pap in pl:
                plist.append((l, pap))
    wst = {"cur": 0, "nl": 0}

    def wget():
        while wst["nl"] < len(plist) and wst["nl"] <= wst["cur"] + PREF:
            i = wst["nl"]
            l, pap = plist[i]
            sl = wslots[i % NSLOT]
            dma("sp", sl.v().ap, pap, castkeys[l], sl.v().atoms, f"w{i % NSLOT}")
            wst["nl"] += 1
        sl = wslots[wst["cur"] % NSLOT]
        wst["cur"] += 1
        return sl

    def layer_setup(l):
        copy("dve", eh.v(p=(0, 33)), esk.v((l * 16, l * 16 + 16), p=(0, 33)))
        copy("dve", eh32.v(p=(0, 33)), eh.v(p=(0, 33)))
        tt("dve", el32.v(p=(0, 33)), esk.v((l * 16, l * 16 + 16), p=(0, 33)), eh32.v(p=(0, 33)), ALU.subtract)
        memset("dve", esx.v(p=(0, 33)), 0.0)
        for kh in range(NKV):
            copy("dve", esx.custom(kh * 512, [[256, 2], [128, 2], [1, 128]], p=(0, 1)),
                 eh.custom(kh * 4, [[1, 2], [2, 2], [0, 128]], p=(0, 1), span=4))
            copy("dve", esx.custom(kh * 512, [[256, 2], [128, 2], [1, 128]], p=(32, 33)),
                 el32.custom(kh * 4, [[1, 2], [2, 2], [0, 128]], p=(32, 33), span=4))
        dma("sp", gvbc.v().ap, a_gv[l:l + 1, :].partition_broadcast(128).rearrange("p a b -> p (a b)"),
            [], gvbc.v().atoms, "gv")
        dma("sp", wsraw.v().ap, a_ws[l].rearrange("g p q -> p g q"), [], wsraw.v().atoms, "ws")
        for hf in range(2):
            pb = bank()
            for g4 in range(4):
                g = hf * 4 + g4
                transpose(PS.v(pb, (g4 * 128, g4 * 128 + 128)), wsraw.v(g))
            copy("dve", WsT.v((hf * 4, hf * 4 + 4)), PS.custom(pb * 512, [[128, 4], [1, 128]]))
        memset("dve", bs32.v(p=(0, 33)), 0.0)
        dma("sp", bs32.v(p=(0, 1)).ap, a_bs[l:l + 1, :], [], bs32.v().atoms, "bs")
        dma("sp", bs32.v(p=(32, 33)).ap, a_bs[l:l + 1, :], [], bs32.v().atoms, "bs")
        copy("dve", bhi.v(p=(0, 33)), bs32.v(p=(0, 33)))
        copy("dve", bhi32.v(p=(0, 33)), bhi.v(p=(0, 33)))
        tt("dve", blo32.v(p=(0, 33)), bs32.v(p=(0, 33)), bhi32.v(p=(0, 33)), ALU.subtract)
        copy("dve", bsx.v(p=(0, 33)), bhi.v(p=(0, 33)))
        copy("dve", bsx.v(p=(32, 33)), blo32.v(p=(32, 33)))

    def tile_geom(t):
        ts = t % TPS
        hp = 128 if ts != 0 else 0
        hn = 128 if ts != TPS - 1 else 0
        return t * TT, hp, hn, TT + hp + hn

    def prologue(l, t):
        T0, hp, hn, NE = tile_geom(t)
        xb, hb = xT[t % 2], hT
        src = xs[l % 2][:, :, T0 - hp:T0 - hp + NE].rearrange("c p t -> p c t")
        rk = [("xs", l % 2, b) for b in range((T0 - hp) // 128, (T0 - hp + NE) // 128)]
        dma("sp", xb.v(None, (0, NE)).ap, src, rk, xb.v(None, (0, NE)).atoms, f"xld{t % 2}")
        act(sq.v(None, (0, NE)), xb.v(None, (0, NE)), AF.Square)
        segs = [(0, 512)] + ([(512, NE)] if NE > 512 else [])
        for (e0, e1) in segs:
            pb = bank()
            for c in range(KC):
                mm(PS.v(pb, (0, e1 - e0)), ones_bf.v(), sq.v(c, (e0, e1)), c == 0, c == KC - 1)
            rstd(rs.v((e0, e1)), PS.v(pb, (0, e1 - e0)), 1.0 / D)
        for c in range(KC):
            stt("dve", hb.v(c, (0, NE)), xb.v(c, (0, NE)),
                g1col.v((l * 8 + c, l * 8 + c + 1)), rs.v((0, NE)), ALU.mult, ALU.mult)

    slopes = [float(2.0 ** (-8.0 * (h + 1) / NH)) for h in range(NH)]

    def body(l, t, last):
        T0, hp, hn, NE = tile_geom(t)
        NEB = NE // 128
        xb, hb = xT[t % 2], hT
        cen = (hp, hp + TT)
        gvb = Ring(gvbP[t % 2])
        vpp = vppP[t % 2]

        def fm_proj(wp, oc4, rhs_buf, rhs_cols):
            pb = bank()
            for kc in range(KC):
                mm(PS.v(pb), wp.v(kc, (oc4 * 128, oc4 * 128 + 128)), rhs_buf.v(kc, rhs_cols), kc == 0, kc == KC - 1)
            return pb

        ck(6)
        pend = []

        def qk_norm_start(pa, n, gcol, dst):
            s_ = bfr.next()
            act(s_.v((0, n)), PS.v(pa, (0, n)), AF.Square)
            pend.append((pa, n, gcol, dst, s_))

        def qk_norm_flush():
            while pend:
                pa, n, gcol, dst, s_ = pend.pop(0)
                pb2 = bank()
                mm(PS.v(pb2, (0, n)), bd_bf.v(), s_.v((0, n)), True, True)
                r_ = f32r.next()
                rstd(r_.v((0, n)), PS.v(pb2, (0, n)), 1.0 / HD)
                if isinstance(dst, tuple):
                    kh_, e0_, e1_ = dst
                    for hf_ in range(2):
                        pp_ = (hf_ * 64, hf_ * 64 + 64)
                        stt("dve", KT.v(kh_, hf_, (e0_, e1_), p=pp_), PS.v(pa, (0, n), p=pp_),
                            gcol.v((l, l + 1), p=pp_), r_.v((0, n), p=pp_), ALU.mult, ALU.mult)
                else:
                    stt("dve", dst, PS.v(pa, (0, n)), gcol, r_.v((0, n)), ALU.mult, ALU.mult)

        for pi in range(2):
            wp = wget()
            for oc4 in range(4):
                oc = pi * 4 + oc4
                pa = fm_proj(wp, oc4, hb, cen)
                qk_norm_flush()
                qk_norm_start(pa, TT, gqcol.v((l, l + 1)), QT.v(oc))
        wp = wget()
        segs = [(0, 512)] + ([(512, NE)] if NE > 512 else [])
        memset("pool", KT.custom(768, [[1536, 4], [1, 768]], p=(0, 64)), 0.0)
        memset("pool", KT.custom(0, [[1536, 4], [1, 768]], p=(64, 128)), 0.0)
        for kc in range(KC):
            copy("pool", wkd.custom(kc * 512, [[128, 4], [64, 2], [1, 64]]),
                 wp.custom(kc * 512, [[64, 4], [0, 2], [1, 64]], span=256))
        for kh in range(NKV):
            for (e0, e1) in segs:
                n = e1 - e0
                pa = bank()
                for kc in range(KC):
                    mm(PS.v(pa, (0, n)), wkd.v(kc, kh), hb.v(kc, (e0, e1)), kc == 0, kc == KC - 1)
                qk_norm_flush()
                qk_norm_start(pa, n, gkcol, (kh, e0, e1))
        for eb in range(NEB):
            pv = bank()
            for kc in range(KC):
                mm(PS.v(pv, (0, 256)), hb.v(kc, (eb * 128, eb * 128 + 128)), wp.v(kc, (256, 512)), kc == 0, kc == KC - 1)
            if eb == 0:
                qk_norm_flush()
            copy("act" if eb % 2 == 0 else "dve", VD.custom(eb * NKV * 128, [[128, 4], [64, 2], [1, 64]]),
                 PS.custom(pv * 512, [[64, 4], [0, 2], [1, 64]], span=256))
        qk_norm_flush()

        items = []
        wst_ = {}

        def mk_fm_item(pi, oc4):
            def item():
                if oc4 == 0:
                    wst_[pi] = wget()
                wp = wst_[pi]
                dst, fn = ((sga, AF.Sigmoid), (sgb, AF.Sigmoid), (um, AF.Gelu))[pi // 2]
                oc = (pi % 2) * 4 + oc4
                pb = fm_proj(wp, oc4, hb, cen)
                act(dst.v(oc), PS.v(pb), fn)
            return item

        for pi in range(6):
            for oc4 in range(4):
                items.append(mk_fm_item(pi, oc4))

        def mk_gv_item(b):
            def item():
                if b == 0:
                    wst_["g6"] = wget()
                    wst_["g7"] = wget()
                g = gvb.next()
                c0 = hp + b * 128
                for hf, wp in enumerate((wst_["g6"], wst_["g7"])):
                    pb = bank()
                    for kc in range(KC):
                        mm(PS.v(pb), hb.v(kc, (c0, c0 + 128)), wp.v(kc), kc == 0, kc == KC - 1)
                    act(g.v((hf * 512, hf * 512 + 512)), PS.v(pb), AF.Gelu)
                act(junk.v(), g.v(), AF.Square)
                sc.add("dve", lambda e, o=ssb.v((b, b + 1)).ap, i=junk.v().ap: e.reduce_sum(out=o, in_=i, axis=AX.X),
                       r=junk.v().atoms, w=ssb.v((b, b + 1)).atoms)
                rstd(ssb.v((4 + b, 5 + b)), ssb.v((b, b + 1)), 1.0 / 1024)
                stt("dve", vpp.v(b), g.v(), ssb.v((4 + b, 5 + b)), gvbc.v(), ALU.mult, ALU.mult)
            return item

        for b in range(NBLK):
            items.append(mk_gv_item(b))

        def mk_sp_item(b, gh):
            def item():
                pb = bank()
                for gi in range(4):
                    g = gh * 4 + gi
                    out = PS.v(pb, (gi * 128, gi * 128 + 128))
                    mm(out, vpp.v(b, (g * 128, g * 128 + 128)), WsT.v(g), True, False)
                    mm(out, ones33.v(p=(0, 33)), bsx.v((g * 128, g * 128 + 128), p=(0, 33)), False, True)
                u_ = um.custom(gh * 4 * TT + b * 128, [[TT, 4], [1, 128]])
                tt("dve", u_, PS.custom(pb * 512, [[128, 4], [1, 128]]), u_, ALU.mult)
            return item

        for b in range(NBLK):
            for gh in range(2):
                items.append(mk_sp_item(b, gh))
        ck(7)
        ck(8)
        units = [(b, kh) for b in range(NBLK) for kh in range(NKV)]
        stA = {}

        def stageA(u):
            b, kh = u
            qb = b + hp // 128
            pts = []
            for j in range(3):
                kb = qb + j - 1
                if kb < 0 or kb >= NEB:
                    continue
                ps_s = bank()
                for g in range(GQ):
                    hf, gg = g % 2, g // 2
                    c0 = hf * 256 + gg * 128
                    mm(PS.v(ps_s, (c0, c0 + 128)), KT.v(kh, hf, (kb * 128, kb * 128 + 128)),
                       QT.v(2 * kh + gg, (b * 128, b * 128 + 128)), True, True)
                ck(8.1)
                pt = bfr.next()
                act(pt.v(), PS.v(ps_s), AF.Exp, scale=HD ** -0.5)
                ck(8.2)
                tt("pool", pt.v(), pt.v(), MT.v(kh, j), ALU.mult)
                ck(8.3)
                pts.append((kb, pt))
            stA[u] = pts

        def stageB(u):
            b, kh = u
            pts = stA.pop(u)
            po, pd = bank(), bank()
            for i, (kb, pt) in enumerate(pts):
                mm(PS.v(po), VD.v(kb, kh), pt.v(), i == 0, i == len(pts) - 1)
                mm(PS.v(pd), ones_bf.v(), pt.v(), i == 0, False)
            mm(PS.v(pd), ones33.v(p=(0, 33)), esx.v(kh, p=(0, 33)), False, True)
            ck(8.4)
            rd = f32r.next()
            act(rd.v(), PS.v(pd), AF.Ln)
            act(rd.v(), rd.v(), AF.Exp, scale=-1.0)
            ck(8.5)
            for hf in range(2):
                pp = (hf * 64, hf * 64 + 64)
                o_ = OT.custom(2 * kh * TT + b * 128, [[TT, 2], [1, 128]], p=pp)
                tt("dve", o_, PS.custom(po * 512 + hf * 256, [[128, 2], [1, 128]], p=pp),
                   rd.custom(hf * 256, [[128, 2], [1, 128]], p=pp), ALU.mult)

        LAG = 2
        nit = len(items)
        done = 0
        for i, u in enumerate(units):
            stageA(u)
            if i >= LAG:
                stageB(units[i - LAG])
            tgt = (i + 1) * nit // len(units)
            while done < tgt:
                items[done]()
                done += 1
        for u in units[-LAG:]:
            stageB(u)
        while done < nit:
            items[done]()
            done += 1
        ck(9)
        for hf in range(2):
            wpa = wget()
            wpb = wget()
            for oc4 in range(4):
                oc = hf * 4 + oc4
                pa = fm_proj(wpa, oc4, um, (0, TT))
                pb = fm_proj(wpb, oc4, OT, (0, TT))
                ta = f32r.next()
                tb = f32r.next()
                tt("dve", ta.v(), PS.v(pa), sga.v(oc), ALU.mult)
                tt("dve", tb.v(), PS.v(pb), sgb.v(oc), ALU.mult)
                tt("pool", mixT.v(oc), ta.v(), tb.v(), ALU.add)
        for hf in range(2):
            wp = wget()
            for oc4 in range(4):
                oc = hf * 4 + oc4
                pb = fm_proj(wp, oc4, mixT, (0, TT))
                tt("dve", xb.v(oc, cen), PS.v(pb), xb.v(oc, cen), ALU.add)
        ck(10)
        act(sq.v(None, (0, TT)), xb.v(None, cen), AF.Square)
        pb = bank()
        for c in range(KC):
            mm(PS.v(pb), ones_bf.v(), sq.v(c, (0, TT)), c == 0, c == KC - 1)
        rstd(rs.v((0, TT)), PS.v(pb), 1.0 / D)
        for c in range(KC):
            stt("dve", h2T.v(c), xb.v(c, cen),
                g2col.v((l * 8 + c, l * 8 + c + 1)), rs.v((0, TT)), ALU.mult, ALU.mult)
        for pi in range(8):
            wp = wget()
            for oc4 in range(4):
                fc = pi * 4 + oc4
                pb = fm_proj(wp, oc4, h2T, (0, TT))
                rl = f32r.next()
                act(rl.v(), PS.v(pb), AF.Relu)
                tt("pool", actT.v(fc), rl.v(), rl.v(), ALU.mult)
        ck(11)
        return (l, t, last)

    def body2(l, t, last):
        T0, hp, hn, NE = tile_geom(t)
        xb = xT[t % 2]
        cen = (hp, hp + TT)
        for hf in range(2):
            banks4 = [bank() for _ in range(4)]
            for q in range(4):
                wp = wget()
                for oc4 in range(4):
                    for kc in range(KC):
                        mm(PS.v(banks4[oc4]), wp.v(kc, (oc4 * 128, oc4 * 128 + 128)), actT.v(q * 8 + kc),
                           q == 0 and kc == 0, q == 3 and kc == KC - 1)
            for oc4 in range(4):
                oc = hf * 4 + oc4
                tt("dve", xb.v(oc, cen), PS.v(banks4[oc4]), xb.v(oc, cen), ALU.add)
        if not last:
            dst = xs[(l + 1) % 2][:, :, T0:T0 + TT].rearrange("c p t -> p c t")
            wk = [("xs", (l + 1) % 2, b) for b in range(T0 // 128, (T0 + TT) // 128)]
            dma("sp", dst, xb.v(None, cen).ap, xb.v(None, cen).atoms, wk, f"xst{t % 2}")
        else:
            for b in range(NBLK):
                k = xinr.i % 2
                xi = xinr.next()
                for hf in range(2):
                    pb = bank()
                    for c4 in range(4):
                        c = hf * 4 + c4
                        transpose(PS.v(pb, (c4 * 128, c4 * 128 + 128)), xb.v(c, (hp + b * 128, hp + b * 128 + 128)))
                    copy("act" if hf == 0 else "dve", xi.v((hf * 512, hf * 512 + 512)), PS.v(pb))
                r0 = T0 + b * 128
                dma("sp", y_out[r0:r0 + 128, :], xi.v().ap, xi.v().atoms, [("y", r0 // 128)], f"yst{k}")

    for l in range(L):
        layer_setup(l)
        ck(4)
        prologue(l, 0)
        ck(5)
        for t in range(NT):
            st = body(l, t, l == L - 1)
            if t + 1 < NT:
                prologue(l, t + 1)
            body2(*st)
    sc.add("sp", None, r=[("y", b) for b in range(NBT)])

    sc.finalize()
    dnames = sorted(sc.dma_count.keys())
    dsem = {n: es.enter_context(nc.semaphore("d_" + n)) for n in dnames}
    esem = {e: es.enter_context(nc.semaphore("e_" + e)) for e in ("pe", "act", "dve", "pool", "sp")}
    counts = {}
    with nc.Block() as block:
        @block.tensor
        def _(h):
            counts["pe"] = sc.emit_engine("pe", h, esem, dsem)

        @block.scalar
        def _(h):
            counts["act"] = sc.emit_engine("act", h, esem, dsem)

        @block.vector
        def _(h):
            counts["dve"] = sc.emit_engine("dve", h, esem, dsem)

        @block.gpsimd
        def _(h):
            counts["pool"] = sc.emit_engine("pool", h, esem, dsem)

        @block.sync
        def _(h):
            counts["sp"] = sc.emit_engine("sp", h, esem, dsem)
    es.close()
    kinfo = dict(counts=counts, nops=len(sc.ops), nwait=sc.nwait, sbuf_end=cur[0])
    return nc, kinfo


def _mult_table():
    s_ = np.arange(128)[:, None]
    q_ = np.arange(128)[None, :]
    slopes = np.exp2(-8.0 * (np.arange(NH, dtype=np.float64) + 1.0) / NH)
    out = np.zeros((128, NKV, 3, 2, 2, 128), np.float32)
    for j in range(3):
        dist = np.abs((s_ + (j - 1) * 128) - q_).astype(np.float64)
        valid = dist <= 128
        for kh in range(NKV):
            for hf in range(2):
                for gg in range(2):
                    h = kh * GQ + hf + 2 * gg
                    out[:, kh, j, hf, gg, :] = np.where(valid, np.exp(-slopes[h] * dist), 0.0)
    return out.reshape(128, -1)


def make_in_maps(inputs, ncores, nseq, L):
    x = np.asarray(inputs["x"], np.float32)
    B, S, _ = x.shape
    f = lambda k: np.ascontiguousarray(np.asarray(inputs[k], np.float32)[:L])
    g1 = f("ln1_g").reshape(L, 8, 128).transpose(2, 0, 1).reshape(128, L * 8)
    g2 = f("ln2_g").reshape(L, 8, 128).transpose(2, 0, 1).reshape(128, L * 8)
    gq = np.tile(f("b_q_norm_g").T, (2, 1))
    gk = np.tile(f("b_k_norm_g").T, (2, 1))
    shared = {
        "w_in": f("w_in"), "w_branch_a": f("w_branch_a"), "w_branch_b": f("w_branch_b"),
        "w_out": f("w_out"), "w_ff1": f("w_ff1"), "w_ff2": f("w_ff2"),
        "a_w_s": f("a_w_s"), "a_b_s": f("a_b_s").reshape(L, 1024), "a_norm_g": f("a_norm_g"),
        "b_sink": f("b_sink").reshape(1, L * 16),
        "g1col": np.ascontiguousarray(g1), "g2col": np.ascontiguousarray(g2),
        "gqcol": np.ascontiguousarray(gq), "gkcol": np.ascontiguousarray(gk),
        "c_ident": np.eye(128, dtype=np.float32), "c_mtab": _mult_table(),
    }
    maps = []
    for c in range(ncores):
        m = dict(shared)
        m["x"] = np.ascontiguousarray(x[c * nseq:(c + 1) * nseq].reshape(nseq * S, D))
        maps.append(m)
    return maps


def kernel(**inputs):
    x = np.asarray(inputs["x"])
    B, S, _ = x.shape
    L = np.asarray(inputs["w_in"]).shape[0]
    ncores = 8
    nseq = B // ncores
    nc, _ = build_program(nseq, S, L)
    maps = make_in_maps(inputs, ncores, nseq, L)
    res = run_bass_kernel_spmd(nc, maps, core_ids=list(range(ncores)))
    outs = [np.asarray(r["y"]).reshape(nseq, S, D) for r in res.results]
    return np.concatenate(outs, axis=0).astype(np.float32)
```

```python
import contextlib
import numpy as np
import concourse.bass as bass
import concourse.mybir as mybir
from concourse.ap import AP
from concourse.bass_utils import run_bass_kernel_spmd

F32 = mybir.dt.float32
BF16 = mybir.dt.bfloat16
AF = mybir.ActivationFunctionType
ALU = mybir.AluOpType
AX = mybir.AxisListType

D = 1024
KC = 8
TT = 512
NBLK = 4
HD = 64
NH = 16
NKV = 4
GQ = 4
DFF = 4096
INW = 5632
EPS = 1e-6
ATOM = 512
NSLOT = 4
PREF = 2


class Sched:
    def __init__(self):
        self.ops = []
        self.lastw = {}
        self.lastr = {}
        self.dma_count = {}
        self.stopped = False

    def add(self, eng, fn, r=(), w=(), dma=None, bulk=False):
        if self.stopped:
            return -1
        i = len(self.ops)
        deps = set()
        lastw, lastr = self.lastw, self.lastr
        for a in r:
            j = lastw.get(a)
            if j is not None:
                deps.add(j)
        for a in w:
            j = lastw.get(a)
            if j is not None:
                deps.add(j)
            rd = lastr.get(a)
            if rd:
                deps.update(rd.values())
        ekey = ("d", dma) if dma else eng
        for a in w:
            lastw[a] = i
            lastr[a] = {}
        for a in r:
            d = lastr.get(a)
            if d is None:
                d = lastr[a] = {}
            d[ekey] = i
        deps.discard(i)
        o = 0
        if dma:
            o = self.dma_count[dma] = self.dma_count.get(dma, 0) + 1
        self.ops.append([eng, fn, deps, dma, o, bulk, False, 0, None, None])
        return i

    def finalize(self):
        ops = self.ops
        for op in ops:
            eng = op[0]
            nd = set()
            for j in op[2]:
                y = ops[j]
                if y[3] is None and y[0] == eng and eng in ("pe", "sp"):
                    continue
                nd.add(j)
                if y[3] is None:
                    y[6] = True
            op[2] = nd
        cnt = {}
        for op in ops:
            if op[3] is None and op[6]:
                cnt[op[0]] = cnt.get(op[0], 0) + 1
                op[7] = cnt[op[0]]
        vc = {}
        nwait = 0
        for op in ops:
            e = op[0]
            my = vc.setdefault(e, {})
            waits = {}
            for j in op[2]:
                y = ops[j]
                if y[3] is not None:
                    key = ("d", y[3])
                    val = 16 * (self.dma_count[y[3]] if y[5] else y[4])
                else:
                    key = ("e", y[0])
                    val = y[7]
                if my.get(key, 0) >= val:
                    continue
                if waits.get(key, 0) < val:
                    waits[key] = val
            for key, val in waits.items():
                my[key] = val
            for j in op[2]:
                s = ops[j][9]
                if s:
                    for k, v in s.items():
                        if my.get(k, 0) < v:
                            my[k] = v
            op[8] = list(waits.items())
            nwait += len(op[8])
            if op[3] is not None:
                snap = dict(my)
                if not op[5]:
                    snap[("d", op[3])] = max(snap.get(("d", op[3]), 0), 16 * op[4])
                op[9] = snap
            elif op[6]:
                snap = dict(my)
                snap[("e", e)] = op[7]
                op[9] = snap
        self.nwait = nwait

    def emit_engine(self, eng, h, esem, dsem):
        def semof(key):
            return dsem[key[1]] if key[0] == "d" else esem[key[1]]

        n = 0
        for op in self.ops:
            if op[0] != eng:
                continue
            waits = op[8]
            fn = op[1]
            if fn is None or op[3] is not None:
                for key, val in waits:
                    h.wait_ge(semof(key), val)
                    n += 1
                if fn is None:
                    continue
                ins = fn(h)
            else:
                for key, val in waits[1:]:
                    h.wait_ge(semof(key), val)
                    n += 1
                ins = fn(h)
                if waits:
                    ins._wait_ge(semof(waits[0][0]), waits[0][1])
            n += 1
            if op[3] is not None:
                ins.then_inc(dsem[op[3]], 16)
            elif op[6]:
                ins.then_inc(esem[eng], 1)
        return n


class Acc:
    __slots__ = ("ap", "atoms")

    def __init__(self, ap, atoms):
        self.ap = ap
        self.atoms = atoms


class Buf:
    def __init__(self, name, t, fs, esz, track=None, base=0, pstep=None, eoff=0, atom=ATOM):
        self.name = name
        self.t = t
        self.fs = tuple(fs)
        self.esz = esz
        st = []
        s = 1
        for n in reversed(self.fs):
            st.append(s)
            s *= n
        self.st = tuple(reversed(st))
        self.pstep = pstep or s
        self.eoff = eoff
        self.track = track or name
        self.base = base
        self.cache = {}
        self.atom = atom

    def v(self, *idx, p=(0, 128)):
        key = (idx, p)
        c = self.cache.get(key)
        if c is not None:
            return c
        assert len(idx) <= len(self.fs)
        idx = tuple(idx) + (None,) * (len(self.fs) - len(idx))
        off = p[0] * self.pstep + self.eoff
        dims = [[self.pstep, p[1] - p[0]]]
        rng = []
        for i, ix in enumerate(idx):
            if ix is None:
                lo, hi = 0, self.fs[i]
            elif isinstance(ix, tuple):
                lo, hi = ix
            else:
                off += ix * self.st[i]
                rng.append((self.st[i], ix, ix + 1))
                continue
            assert 0 <= lo < hi <= self.fs[i], (self.name, idx)
            off += lo * self.st[i]
            dims.append([self.st[i], hi - lo])
            rng.append((self.st[i], lo, hi))
        ap = AP(self.t, off, dims)
        atoms = self._atoms(rng)
        c = Acc(ap, atoms)
        self.cache[key] = c
        return c

    def _atoms(self, rng):
        runs = [(0,)]
        starts = [0]
        for (st, lo, hi) in rng[:-1]:
            starts = [s + st * k for s in starts for k in range(lo, hi)]
        st, lo, hi = rng[-1]
        atoms = set()
        for s in starts:
            b0 = self.base + (s + st * lo) * self.esz
            b1 = self.base + (s + st * (hi - 1) + 1) * self.esz
            for a in range(b0 // self.atom, (b1 - 1) // self.atom + 1):
                atoms.add((self.track, a))
        return frozenset(atoms)

    def custom(self, off_elems, dims, p=(0, 128), span=None):
        off = p[0] * self.pstep + self.eoff + off_elems
        ap = AP(self.t, off, [[self.pstep, p[1] - p[0]]] + [list(d) for d in dims])
        if span is None:
            span = 1 + sum(s * (n - 1) for s, n in dims)
        b0 = self.base + off_elems * self.esz
        b1 = self.base + (off_elems + span) * self.esz
        atoms = frozenset((self.track, a) for a in range(b0 // self.atom, (b1 - 1) // self.atom + 1))
        return Acc(ap, atoms)


class Ring:
    def __init__(self, bufs):
        self.bufs = bufs
        self.i = 0

    def next(self):
        b = self.bufs[self.i % len(self.bufs)]
        self.i += 1
        return b


SB_LO = 16512
SB_HI = 229344


def build_program(nseq, S, L):
    NTOK = nseq * S
    NT = NTOK // TT
    TPS = S // TT
    NBT = NTOK // 128
    nc = bass.Bass("TRN2", target_bir_lowering=False)
    sc = Sched()
    import os as _os
    kstop = float(_os.environ.get("KSTOP", "99"))

    def ck(n):
        if n >= kstop:
            sc.stopped = True
    es = contextlib.ExitStack()

    def dram(name, shape, dt, kind):
        return nc.dram_tensor(name, shape, dt, kind=kind).ap()

    x_in = dram("x", [NTOK, D], F32, "ExternalInput")
    y_out = dram("y", [NTOK, D], F32, "ExternalOutput")
    w_in = dram("w_in", [L, D, INW], F32, "ExternalInput")
    w_a = dram("w_branch_a", [L, D, D], F32, "ExternalInput")
    w_b = dram("w_branch_b", [L, D, D], F32, "ExternalInput")
    w_o = dram("w_out", [L, D, D], F32, "ExternalInput")
    w_f1 = dram("w_ff1", [L, D, DFF], F32, "ExternalInput")
    w_f2 = dram("w_ff2", [L, DFF, D], F32, "ExternalInput")
    a_ws = dram("a_w_s", [L, 8, 128, 128], F32, "ExternalInput")
    a_bs = dram("a_b_s", [L, 1024], F32, "ExternalInput")
    a_gv = dram("a_norm_g", [L, 1024], F32, "ExternalInput")
    sink = dram("b_sink", [1, L * 16], F32, "ExternalInput")
    g1c = dram("g1col", [128, L * 8], F32, "ExternalInput")
    g2c = dram("g2col", [128, L * 8], F32, "ExternalInput")
    gqc = dram("gqcol", [128, L], F32, "ExternalInput")
    gkc = dram("gkcol", [128, L], F32, "ExternalInput")
    c_ident = dram("c_ident", [128, 128], F32, "ExternalInput")
    c_mtab = dram("c_mtab", [128, NKV * 3 * 512], F32, "ExternalInput")
    xs = [dram(f"xs{i}", [KC, 128, NTOK], F32, "Internal") for i in range(2)]
    win_bf = dram("win_bf", [L, D, INW], BF16, "Internal")
    wa_bf = dram("wa_bf", [L, D, D], BF16, "Internal")
    wb_bf = dram("wb_bf", [L, D, D], BF16, "Internal")
    wo_bf = dram("wo_bf", [L, D, D], BF16, "Internal")
    f1_bf = dram("f1_bf", [L, D, DFF], BF16, "Internal")
    f2_bf = dram("f2_bf", [L, DFF, D], BF16, "Internal")

    cur = [SB_LO]

    def sb(name, fs, dt, at=None):
        esz = 2 if dt == BF16 else 4
        n = 1
        for k in fs:
            n *= k
        nbytes = n * esz
        if at is None:
            at = cur[0]
            cur[0] = (at + nbytes + 31) // 32 * 32
            assert cur[0] <= SB_HI, ("SBUF overflow", name, cur[0])
        t = nc.alloc_sbuf_tensor_at(name, [128] + list(fs), dt, offset=at)
        return Buf(name, t, fs, esz, track="SB", base=at)

    xT = [sb(f"xT{i}", [KC, 768], F32) for i in range(2)]
    hT = sb("hT", [KC, 768], BF16)
    sq = sb("sq", [KC, 768], BF16)
    rs = sb("rs", [768], F32)
    sga = sb("sga", [KC, TT], BF16)
    sgb = sb("sgb", [KC, TT], BF16)
    um = sb("um", [KC, TT], BF16)
    OT = sb("OT", [KC, TT], BF16, at=sq.base)
    h2T = Buf("h2T", sga.t, [KC, TT], 2, track="SB", base=sga.base)
    wkd = sb("wkd", [KC, NKV, 128], BF16, at=sq.base)
    A0 = cur[0]
    cur[0] += 32768
    actT = sb("actT", [32, TT], BF16, at=A0)
    QT = sb("QT", [KC, TT], BF16, at=A0)
    KT = sb("KT", [NKV, 2, 768], BF16, at=A0 + 8192)
    VD = sb("VD", [6, NKV, 128], BF16, at=A0 + 20480)
    gvbP = [[sb(f"gvb0_{p}", [1024], F32, at=xT[1 - p].base), sb(f"gvb1_{p}", [1024], F32, at=xT[1 - p].base + 4096)]
            for p in range(2)]
    vppP = [sb(f"vpp_{p}", [NBLK, 1024], BF16, at=xT[1 - p].base + 8192) for p in range(2)]
    junk = sb("junk", [1024], F32, at=sq.base + 8192)
    mixT = sb("mixT", [KC, TT], BF16, at=A0)
    wsraw = sb("wsraw", [8, 128], F32, at=A0 + 20480)
    bs32 = sb("bs32", [1024], F32, at=A0)
    bhi32 = sb("bhi32", [1024], F32, at=A0 + 4096)
    blo32 = sb("blo32", [1024], F32, at=A0 + 8192)
    bhi = sb("bhi", [1024], BF16, at=A0 + 12288)
    xinr = Ring([sb("xin0", [1024], F32, at=A0), sb("xin1", [1024], F32, at=A0 + 4096)])
    xtsr = Ring([sb("xts0", [KC, 128], F32, at=A0 + 8192), sb("xts1", [KC, 128], F32, at=A0 + 12288)])
    f32r = Ring([sb(f"f32r{i}", [TT], F32) for i in range(4)])
    bfr = Ring([sb(f"bfr{i}", [TT], BF16) for i in range(10)])
    wslots = [sb(f"wsl{i}", [KC, 512], BF16) for i in range(NSLOT)]
    MT = sb("MT", [NKV, 3, 512], BF16)
    esx = sb("esx", [NKV, 512], BF16)
    eh = sb("eh", [16], BF16)
    eh32 = sb("eh32", [16], F32)
    el32 = sb("el32", [16], F32)
    gvbc = sb("gvbc", [1024], F32)
    WsT = sb("WsT", [8, 128], BF16)
    bsx = sb("bsx", [1024], BF16)
    ident = sb("ident", [128], F32)
    ones_bf = sb("ones_bf", [128], BF16)
    bd_bf = sb("bd_bf", [128], BF16)
    ones33 = sb("ones33", [128], BF16)
    g1col = sb("g1col_s", [L * 8], F32)
    g2col = sb("g2col_s", [L * 8], F32)
    gqcol = sb("gqcol_s", [L], F32)
    gkcol = sb("gkcol_s", [L], F32)
    eskraw = sb("eskraw", [L * 16], F32)
    esk = sb("esk", [L * 16], F32)
    ssb = sb("ssb", [8], F32)
    eps_col = sb("eps_col", [1], F32)

    pst = es.enter_context(nc.psum_tensor("ps", [128, 8, 512], F32))
    PS = Buf("ps", pst, [8, 512], 4, track="PS", atom=2048)
    bank_ctr = [0]

    def bank():
        b = bank_ctr[0] % 8
        bank_ctr[0] += 1
        return b

    def mm(out, lhsT, rhs, start, stop):
        sc.add("pe", lambda e, o=out.ap, l=lhsT.ap, r=rhs.ap, s0=start, s1=stop:
               e.matmul(o, lhsT=l, rhs=r, start=s0, stop=s1),
               r=lhsT.atoms | rhs.atoms, w=out.atoms)

    def act(out, in_, func, scale=1.0, bias=None):
        if bias is None:
            sc.add("act", lambda e, o=out.ap, i=in_.ap, f=func, s=scale:
                   e.activation(out=o, in_=i, func=f, scale=s),
                   r=in_.atoms, w=out.atoms)
        else:
            sc.add("act", lambda e, o=out.ap, i=in_.ap, f=func, s=scale, b=bias.ap:
                   e.activation(out=o, in_=i, func=f, scale=s, bias=b),
                   r=in_.atoms | bias.atoms, w=out.atoms)

    def tt(eng, out, in0, in1, op):
        sc.add(eng, lambda e, o=out.ap, a=in0.ap, b=in1.ap, op=op:
               e.tensor_tensor(out=o, in0=a, in1=b, op=op),
               r=in0.atoms | in1.atoms, w=out.atoms)

    def stt(eng, out, in0, scalar, in1, op0, op1):
        if isinstance(scalar, Acc):
            sc.add(eng, lambda e, o=out.ap, a=in0.ap, s=scalar.ap, b=in1.ap, op0=op0, op1=op1:
                   e.scalar_tensor_tensor(out=o, in0=a, scalar=s, in1=b, op0=op0, op1=op1),
                   r=in0.atoms | in1.atoms | scalar.atoms, w=out.atoms)
        else:
            sc.add(eng, lambda e, o=out.ap, a=in0.ap, s=float(scalar), b=in1.ap, op0=op0, op1=op1:
                   e.scalar_tensor_tensor(out=o, in0=a, scalar=s, in1=b, op0=op0, op1=op1),
                   r=in0.atoms | in1.atoms, w=out.atoms)

    def recip(eng, out, in_):
        sc.add(eng, lambda e, o=out.ap, i=in_.ap: e.reciprocal(out=o, in_=i),
               r=in_.atoms, w=out.atoms)

    def copy(eng, out, in_):
        if eng == "act":
            sc.add("act", lambda e, o=out.ap, i=in_.ap: e.copy(out=o, in_=i), r=in_.atoms, w=out.atoms)
        else:
            sc.add(eng, lambda e, o=out.ap, i=in_.ap: e.tensor_copy(out=o, in_=i), r=in_.atoms, w=out.atoms)

    def memset(eng, out, val):
        sc.add(eng, lambda e, o=out.ap, v=val: e.memset(o, v), w=out.atoms)

    def dma(eng, out_ap, in_ap, r, w, sem, bulk=False):
        sc.add(eng, lambda e, o=out_ap, i=in_ap: e.dma_start(out=o, in_=i), r=r, w=w, dma=sem, bulk=bulk)

    def transpose(out, in_):
        sc.add("pe", lambda e, o=out.ap, i=in_.ap, idn=ident.v().ap: e.transpose(o, i, idn),
               r=in_.atoms | ident.v().atoms, w=out.atoms)

    def rstd(out, in_, scale):
        act(out, in_, AF.Ln, scale=scale, bias=eps_col.v())
        act(out, out, AF.Exp, scale=-0.5)

    memset("dve", eps_col.v(), EPS)
    memset("dve", ones_bf.v(), 1.0)
    memset("dve", bd_bf.v(), 0.0)
    memset("dve", bd_bf.v((0, 64), p=(0, 64)), 1.0)
    memset("dve", bd_bf.v((64, 128), p=(64, 128)), 1.0)
    memset("dve", ones33.v(), 0.0)
    memset("dve", ones33.v(p=(0, 1)), 1.0)
    memset("dve", ones33.v(p=(32, 33)), 1.0)
    dma("sp", ident.v().ap, c_ident, [], ident.v().atoms, "c0")
    dma("sp", g1col.v().ap, g1c, [], g1col.v().atoms, "c2")
    dma("sp", g2col.v().ap, g2c, [], g2col.v().atoms, "c3")
    dma("sp", gqcol.v().ap, gqc, [], gqcol.v().atoms, "c4")
    dma("sp", gkcol.v().ap, gkc, [], gkcol.v().atoms, "c5")
    dma("sp", eskraw.v().ap, sink.partition_broadcast(128).rearrange("p a b -> p (a b)"), [], eskraw.v().atoms, "c6")
    act(esk.v(), eskraw.v(), AF.Exp)
    ck(1)

    for blk in range(NBT):
        xi = xinr.next()
        dma("sp", xi.v().ap, x_in[blk * 128:(blk + 1) * 128, :], [], xi.v().atoms, f"xin{blk % 2}")
        xo = xtsr.next()
        for hf in range(2):
            pb = bank()
            for c4 in range(4):
                c = hf * 4 + c4
                transpose(PS.v(pb, (c4 * 128, c4 * 128 + 128)), xi.v((c * 128, c * 128 + 128)))
            copy("act" if hf == 0 else "dve", xo.v((hf * 4, hf * 4 + 4)), PS.custom(pb * 512, [[128, 4], [1, 128]]))
        dma("sp", xs[0][:, :, blk * 128:(blk + 1) * 128].rearrange("c p t -> p c t"), xo.v().ap,
            xo.v().atoms, [("xs", 0, blk)], f"xts{blk % 2}")

    ck(2)
    NCS = 6
    CW = 2048
    cst_in = [sb(f"cst_in{i}", [CW], F32, at=xT[0].base + i * 12288) for i in range(NCS)]
    cst_out = [sb(f"cst_out{i}", [CW], BF16, at=xT[0].base + i * 12288 + 8192) for i in range(NCS)]
    assert xT[0].base + NCS * 12288 <= sq.base + 12288
    for kh in range(NKV):
        a = cst_in[kh % NCS]
        dma("sp", a.v((0, 1536)).ap, c_mtab[:, kh * 1536:(kh + 1) * 1536], [], a.v((0, 1536)).atoms, f"cin{kh % NCS}")
        copy("dve", MT.custom(kh * 1536, [[1, 1536]]), a.v((0, 1536)))
    castkeys = {}
    chunks = []
    for l in range(L):
        castkeys[l] = []
        for (dst, src, rows, cols, gcol) in ((win_bf[l], w_in[l], D, INW, g1col), (wa_bf[l], w_a[l], D, D, None),
                                             (wb_bf[l], w_b[l], D, D, None), (wo_bf[l], w_o[l], D, D, None),
                                             (f1_bf[l], w_f1[l], D, DFF, g2col), (f2_bf[l], w_f2[l], DFF, D, None)):
            for r0 in range(0, rows, 128):
                for c0 in range(0, cols, CW):
                    cw = min(CW, cols - c0)
                    k = ("wbf", l, len(castkeys[l]))
                    castkeys[l].append(k)
                    gs = None if gcol is None else gcol.v((l * 8 + r0 // 128, l * 8 + r0 // 128 + 1))
                    chunks.append((dst[r0:r0 + 128, c0:c0 + cw], src[r0:r0 + 128, c0:c0 + cw], cw, k, gs))
    AHEAD = 3

    def cast_store(n):
        dst_ap, src_ap, cw, k, gs = chunks[n]
        b = cst_out[(n + NKV) % NCS]
        dma("act", dst_ap, b.v((0, cw)).ap, b.v((0, cw)).atoms, [k], f"cout{(n + NKV) % NCS}")

    for n, (dst_ap, src_ap, cw, k, gs) in enumerate(chunks):
        i = (n + NKV) % NCS
        a, b = cst_in[i], cst_out[i]
        dma("sp", a.v((0, cw)).ap, src_ap, [], a.v((0, cw)).atoms, f"cin{i}")
        if gs is None:
            copy(("dve", "act")[n % 2], b.v((0, cw)), a.v((0, cw)))
        elif n % 2 == 0:
            sc.add("dve", lambda e, o=b.v((0, cw)).ap, i_=a.v((0, cw)).ap, g_=gs.ap: e.tensor_scalar_mul(out=o, in0=i_, scalar1=g_),
                   r=a.v((0, cw)).atoms | gs.atoms, w=b.v((0, cw)).atoms)
        else:
            sc.add("act", lambda e, o=b.v((0, cw)).ap, i_=a.v((0, cw)).ap, g_=gs.ap: e.activation(out=o, in_=i_, func=AF.Copy, scale=g_),
                   r=a.v((0, cw)).atoms | gs.atoms, w=b.v((0, cw)).atoms)
        if n >= AHEAD:
            cast_store(n - AHEAD)
    for n in range(max(0, len(chunks) - AHEAD), len(chunks)):
        cast_store(n)

    ck(3)
    def wpanels(l):
        lst = []
        wv = win_bf[l].rearrange("(c p) n -> p c n", p=128)
        for j in (8, 9, 10, 0, 1, 2, 3, 4, 5, 6, 7):
            lst.append(wv[:, :, j * 512:(j + 1) * 512])
        va = wa_bf[l].rearrange("(c p) n -> p c n", p=128)
        vb = wb_bf[l].rearrange("(c p) n -> p c n", p=128)
        for j in range(2):
            lst.append(va[:, :, j * 512:(j + 1) * 512])
            lst.append(vb[:, :, j * 512:(j + 1) * 512])
        vo = wo_bf[l].rearrange("(c p) n -> p c n", p=128)
        for j in range(2):
            lst.append(vo[:, :, j * 512:(j + 1) * 512])
        v3 = f1_bf[l].rearrange("(c p) n -> p c n", p=128)
        for j in range(8):
            lst.append(v3[:, :, j * 512:(j + 1) * 512])
        v4 = f2_bf[l].rearrange("(q c p) n -> q p c n", c=8, p=128)
        for hf in range(2):
            for q in range(4):
                lst.append(v4[q][:, :, hf * 512:(hf + 1) * 512])
        return lst

    plist = []
    for l in range(L):
        pl = wpanels(l)
        assert len(pl) == 33
        for t in range(NT):
            for pap in pl:
                plist.append((l, pap))
    wst = {"cur": 0, "nl": 0}

    def wget():
        while wst["nl"] < len(plist) and wst["nl"] <= wst["cur"] + PREF:
            i = wst["nl"]
            l, pap = plist[i]
            sl = wslots[i % NSLOT]
            dma("sp", sl.v().ap, pap, castkeys[l], sl.v().atoms, f"w{i % NSLOT}")
            wst["nl"] += 1
        sl = wslots[wst["cur"] % NSLOT]
        wst["cur"] += 1
        return sl

    def layer_setup(l):
        copy("dve", eh.v(p=(0, 33)), esk.v((l * 16, l * 16 + 16), p=(0, 33)))
        copy("dve", eh32.v(p=(0, 33)), eh.v(p=(0, 33)))
        tt("dve", el32.v(p=(0, 33)), esk.v((l * 16, l * 16 + 16), p=(0, 33)), eh32.v(p=(0, 33)), ALU.subtract)
        memset("dve", esx.v(p=(0, 33)), 0.0)
        for kh in range(NKV):
            copy("dve", esx.custom(kh * 512, [[256, 2], [128, 2], [1, 128]], p=(0, 1)),
                 eh.custom(kh * 4, [[1, 2], [2, 2], [0, 128]], p=(0, 1), span=4))
            copy("dve", esx.custom(kh * 512, [[256, 2], [128, 2], [1, 128]], p=(32, 33)),
                 el32.custom(kh * 4, [[1, 2], [2, 2], [0, 128]], p=(32, 33), span=4))
        dma("sp", gvbc.v().ap, a_gv[l:l + 1, :].partition_broadcast(128).rearrange("p a b -> p (a b)"),
            [], gvbc.v().atoms, "gv")
        dma("sp", wsraw.v().ap, a_ws[l].rearrange("g p q -> p g q"), [], wsraw.v().atoms, "ws")
        for hf in range(2):
            pb = bank()
            for g4 in range(4):
                g = hf * 4 + g4
                transpose(PS.v(pb, (g4 * 128, g4 * 128 + 128)), wsraw.v(g))
            copy("dve", WsT.v((hf * 4, hf * 4 + 4)), PS.custom(pb * 512, [[128, 4], [1, 128]]))
        memset("dve", bs32.v(p=(0, 33)), 0.0)
        dma("sp", bs32.v(p=(0, 1)).ap, a_bs[l:l + 1, :], [], bs32.v().atoms, "bs")
        dma("sp", bs32.v(p=(32, 33)).ap, a_bs[l:l + 1, :], [], bs32.v().atoms, "bs")
        copy("dve", bhi.v(p=(0, 33)), bs32.v(p=(0, 33)))
        copy("dve", bhi32.v(p=(0, 33)), bhi.v(p=(0, 33)))
        tt("dve", blo32.v(p=(0, 33)), bs32.v(p=(0, 33)), bhi32.v(p=(0, 33)), ALU.subtract)
        copy("dve", bsx.v(p=(0, 33)), bhi.v(p=(0, 33)))
        copy("dve", bsx.v(p=(32, 33)), blo32.v(p=(32, 33)))

    def tile_geom(t):
        ts = t % TPS
        hp = 128 if ts != 0 else 0
        hn = 128 if ts != TPS - 1 else 0
        return t * TT, hp, hn, TT + hp + hn

    def prologue(l, t):
        T0, hp, hn, NE = tile_geom(t)
        xb, hb = xT[t % 2], hT
        src = xs[l % 2][:, :, T0 - hp:T0 - hp + NE].rearrange("c p t -> p c t")
        rk = [("xs", l % 2, b) for b in range((T0 - hp) // 128, (T0 - hp + NE) // 128)]
        dma("sp", xb.v(None, (0, NE)).ap, src, rk, xb.v(None, (0, NE)).atoms, f"xld{t % 2}")
        act(sq.v(None, (0, NE)), xb.v(None, (0, NE)), AF.Square)
        segs = [(0, 512)] + ([(512, NE)] if NE > 512 else [])
        for (e0, e1) in segs:
            pb = bank()
            for c in range(KC):
                mm(PS.v(pb, (0, e1 - e0)), ones_bf.v(), sq.v(c, (e0, e1)), c == 0, c == KC - 1)
            rstd(rs.v((e0, e1)), PS.v(pb, (0, e1 - e0)), 1.0 / D)
        for c in range(KC):
            tt("dve" if c % 2 == 0 else "pool", hb.v(c, (0, NE)), xb.v(c, (0, NE)), rs.v((0, NE)), ALU.mult)

    slopes = [float(2.0 ** (-8.0 * (h + 1) / NH)) for h in range(NH)]

    def body(l, t, last):
        T0, hp, hn, NE = tile_geom(t)
        NEB = NE // 128
        xb, hb = xT[t % 2], hT
        cen = (hp, hp + TT)
        gvb = Ring(gvbP[t % 2])
        vpp = vppP[t % 2]

        def fm_proj(wp, oc4, rhs_buf, rhs_cols):
            pb = bank()
            for kc in range(KC):
                mm(PS.v(pb), wp.v(kc, (oc4 * 128, oc4 * 128 + 128)), rhs_buf.v(kc, rhs_cols), kc == 0, kc == KC - 1)
            return pb

        ck(6)
        pend = []

        def qk_norm_start(pa, n, gcol, dst):
            s_ = bfr.next()
            act(s_.v((0, n)), PS.v(pa, (0, n)), AF.Square)
            pend.append((pa, n, gcol, dst, s_))

        def qk_norm_flush():
            while pend:
                pa, n, gcol, dst, s_ = pend.pop(0)
                pb2 = bank()
                mm(PS.v(pb2, (0, n)), bd_bf.v(), s_.v((0, n)), True, True)
                r_ = f32r.next()
                rstd(r_.v((0, n)), PS.v(pb2, (0, n)), 1.0 / HD)
                if isinstance(dst, tuple):
                    kh_, e0_, e1_ = dst
                    for hf_ in range(2):
                        pp_ = (hf_ * 64, hf_ * 64 + 64)
                        stt("dve", KT.v(kh_, hf_, (e0_, e1_), p=pp_), PS.v(pa, (0, n), p=pp_),
                            gcol.v((l, l + 1), p=pp_), r_.v((0, n), p=pp_), ALU.mult, ALU.mult)
                else:
                    stt("dve", dst, PS.v(pa, (0, n)), gcol, r_.v((0, n)), ALU.mult, ALU.mult)

        for pi in range(2):
            wp = wget()
            for oc4 in range(4):
                oc = pi * 4 + oc4
                pa = fm_proj(wp, oc4, hb, cen)
                qk_norm_flush()
                qk_norm_start(pa, TT, gqcol.v((l, l + 1)), QT.v(oc))
        wp = wget()
        segs = [(0, 512)] + ([(512, NE)] if NE > 512 else [])
        memset("pool", KT.custom(768, [[1536, 4], [1, 768]], p=(0, 64)), 0.0)
        memset("pool", KT.custom(0, [[1536, 4], [1, 768]], p=(64, 128)), 0.0)
        for kc in range(KC):
            copy("pool", wkd.custom(kc * 512, [[128, 4], [64, 2], [1, 64]]),
                 wp.custom(kc * 512, [[64, 4], [0, 2], [1, 64]], span=256))
        for kh in range(NKV):
            for (e0, e1) in segs:
                n = e1 - e0
                pa = bank()
                for kc in range(KC):
                    mm(PS.v(pa, (0, n)), wkd.v(kc, kh), hb.v(kc, (e0, e1)), kc == 0, kc == KC - 1)
                qk_norm_flush()
                qk_norm_start(pa, n, gkcol, (kh, e0, e1))
        for eb in range(NEB):
            pv = bank()
            for kc in range(KC):
                mm(PS.v(pv, (0, 256)), hb.v(kc, (eb * 128, eb * 128 + 128)), wp.v(kc, (256, 512)), kc == 0, kc == KC - 1)
            if eb == 0:
                qk_norm_flush()
            copy("act" if eb % 2 == 0 else "dve", VD.custom(eb * NKV * 128, [[128, 4], [64, 2], [1, 64]]),
                 PS.custom(pv * 512, [[64, 4], [0, 2], [1, 64]], span=256))
        qk_norm_flush()

        items = []
        wst_ = {}

        def mk_fm_item(pi, oc4):
            def item():
                if oc4 == 0:
                    wst_[pi] = wget()
                wp = wst_[pi]
                dst, fn = ((sga, AF.Sigmoid), (sgb, AF.Sigmoid), (um, AF.Gelu))[pi // 2]
                oc = (pi % 2) * 4 + oc4
                pb = fm_proj(wp, oc4, hb, cen)
                act(dst.v(oc), PS.v(pb), fn)
            return item

        for pi in range(6):
            for oc4 in range(4):
                items.append(mk_fm_item(pi, oc4))

        def mk_gv_item(b):
            def item():
                if b == 0:
                    wst_["g6"] = wget()
                    wst_["g7"] = wget()
                g = gvb.next()
                c0 = hp + b * 128
                for hf, wp in enumerate((wst_["g6"], wst_["g7"])):
                    pb = bank()
                    for kc in range(KC):
                        mm(PS.v(pb), hb.v(kc, (c0, c0 + 128)), wp.v(kc), kc == 0, kc == KC - 1)
                    act(g.v((hf * 512, hf * 512 + 512)), PS.v(pb), AF.Gelu)
                act(junk.v(), g.v(), AF.Square)
                sc.add("dve", lambda e, o=ssb.v((b, b + 1)).ap, i=junk.v().ap: e.reduce_sum(out=o, in_=i, axis=AX.X),
                       r=junk.v().atoms, w=ssb.v((b, b + 1)).atoms)
                rstd(ssb.v((4 + b, 5 + b)), ssb.v((b, b + 1)), 1.0 / 1024)
                stt("dve", vpp.v(b), g.v(), ssb.v((4 + b, 5 + b)), gvbc.v(), ALU.mult, ALU.mult)
            return item

        for b in range(NBLK):
            items.append(mk_gv_item(b))

        def mk_sp_item(b, gh):
            def item():
                pb = bank()
                for gi in range(4):
                    g = gh * 4 + gi
                    out = PS.v(pb, (gi * 128, gi * 128 + 128))
                    mm(out, vpp.v(b, (g * 128, g * 128 + 128)), WsT.v(g), True, False)
                    mm(out, ones33.v(p=(0, 33)), bsx.v((g * 128, g * 128 + 128), p=(0, 33)), False, True)
                u_ = um.custom(gh * 4 * TT + b * 128, [[TT, 4], [1, 128]])
                tt("dve", u_, PS.custom(pb * 512, [[128, 4], [1, 128]]), u_, ALU.mult)
            return item

        for b in range(NBLK):
            for gh in range(2):
                items.append(mk_sp_item(b, gh))
        ck(7)
        ck(8)
        units = [(b, kh) for b in range(NBLK) for kh in range(NKV)]
        stA = {}

        def stageA(u):
            b, kh = u
            qb = b + hp // 128
            pts = []
            for j in range(3):
                kb = qb + j - 1
                if kb < 0 or kb >= NEB:
                    continue
                ps_s = bank()
                for g in range(GQ):
                    hf, gg = g % 2, g // 2
                    c0 = hf * 256 + gg * 128
                    mm(PS.v(ps_s, (c0, c0 + 128)), KT.v(kh, hf, (kb * 128, kb * 128 + 128)),
                       QT.v(2 * kh + gg, (b * 128, b * 128 + 128)), True, True)
                ck(8.1)
                pt = bfr.next()
                act(pt.v(), PS.v(ps_s), AF.Exp, scale=HD ** -0.5)
                ck(8.2)
                tt("pool", pt.v(), pt.v(), MT.v(kh, j), ALU.mult)
                ck(8.3)
                pts.append((kb, pt))
            stA[u] = pts

        def stageB(u):
            b, kh = u
            pts = stA.pop(u)
            po, pd = bank(), bank()
            for i, (kb, pt) in enumerate(pts):
                mm(PS.v(po), VD.v(kb, kh), pt.v(), i == 0, i == len(pts) - 1)
                mm(PS.v(pd), ones_bf.v(), pt.v(), i == 0, False)
            mm(PS.v(pd), ones33.v(p=(0, 33)), esx.v(kh, p=(0, 33)), False, True)
            ck(8.4)
            rd = f32r.next()
            act(rd.v(), PS.v(pd), AF.Ln)
            act(rd.v(), rd.v(), AF.Exp, scale=-1.0)
            ck(8.5)
            for hf in range(2):
                pp = (hf * 64, hf * 64 + 64)
                o_ = OT.custom(2 * kh * TT + b * 128, [[TT, 2], [1, 128]], p=pp)
                tt("dve", o_, PS.custom(po * 512 + hf * 256, [[128, 2], [1, 128]], p=pp),
                   rd.custom(hf * 256, [[128, 2], [1, 128]], p=pp), ALU.mult)

        LAG = 2
        nit = len(items)
        done = 0
        PRE = 6
        while done < PRE:
            items[done]()
            done += 1
        for i, u in enumerate(units):
            stageA(u)
            if i >= LAG:
                stageB(units[i - LAG])
            tgt = PRE + (i + 1) * (nit - PRE) // len(units)
            while done < tgt:
                items[done]()
                done += 1
        for u in units[-LAG:]:
            stageB(u)
        while done < nit:
            items[done]()
            done += 1
        ck(9)
        for hf in range(2):
            wpa = wget()
            wpb = wget()
            for oc4 in range(4):
                oc = hf * 4 + oc4
                pa = fm_proj(wpa, oc4, um, (0, TT))
                pb = fm_proj(wpb, oc4, OT, (0, TT))
                ta = f32r.next()
                tb = f32r.next()
                tt("dve", ta.v(), PS.v(pa), sga.v(oc), ALU.mult)
                tt("dve", tb.v(), PS.v(pb), sgb.v(oc), ALU.mult)
                tt("pool", mixT.v(oc), ta.v(), tb.v(), ALU.add)
        for hf in range(2):
            wp = wget()
            for oc4 in range(4):
                oc = hf * 4 + oc4
                pb = fm_proj(wp, oc4, mixT, (0, TT))
                tt("dve", xb.v(oc, cen), PS.v(pb), xb.v(oc, cen), ALU.add)
        ck(10)
        for c0 in range(0, KC, 2):
            act(sq.v((c0, c0 + 2), (0, TT)), xb.v((c0, c0 + 2), cen), AF.Square)
        pb = bank()
        for c in range(KC):
            mm(PS.v(pb), ones_bf.v(), sq.v(c, (0, TT)), c == 0, c == KC - 1)
        rstd(rs.v((0, TT)), PS.v(pb), 1.0 / D)
        for c in range(KC):
            tt("dve" if c % 2 == 0 else "pool", h2T.v(c), xb.v(c, cen), rs.v((0, TT)), ALU.mult)
        for pi in range(8):
            wp = wget()
            for oc4 in range(4):
                fc = pi * 4 + oc4
                pb = fm_proj(wp, oc4, h2T, (0, TT))
                rl = f32r.next()
                act(rl.v(), PS.v(pb), AF.Relu)
                tt("pool", actT.v(fc), rl.v(), rl.v(), ALU.mult)
        ck(11)
        return (l, t, last)

    def body2(l, t, last):
        T0, hp, hn, NE = tile_geom(t)
        xb = xT[t % 2]
        cen = (hp, hp + TT)
        for hf in range(2):
            banks4 = [bank() for _ in range(4)]
            for q in range(4):
                wp = wget()
                for oc4 in range(4):
                    for kc in range(KC):
                        mm(PS.v(banks4[oc4]), wp.v(kc, (oc4 * 128, oc4 * 128 + 128)), actT.v(q * 8 + kc),
                           q == 0 and kc == 0, q == 3 and kc == KC - 1)
            for oc4 in range(4):
                oc = hf * 4 + oc4
                tt("dve", xb.v(oc, cen), PS.v(banks4[oc4]), xb.v(oc, cen), ALU.add)
        if not last:
            dst = xs[(l + 1) % 2][:, :, T0:T0 + TT].rearrange("c p t -> p c t")
            wk = [("xs", (l + 1) % 2, b) for b in range(T0 // 128, (T0 + TT) // 128)]
            dma("sp", dst, xb.v(None, cen).ap, xb.v(None, cen).atoms, wk, f"xst{t % 2}")
        else:
            for b in range(NBLK):
                k = xinr.i % 2
                xi = xinr.next()
                for hf in range(2):
                    pb = bank()
                    for c4 in range(4):
                        c = hf * 4 + c4
                        transpose(PS.v(pb, (c4 * 128, c4 * 128 + 128)), xb.v(c, (hp + b * 128, hp + b * 128 + 128)))
                    copy("act" if hf == 0 else "dve", xi.v((hf * 512, hf * 512 + 512)), PS.v(pb))
                r0 = T0 + b * 128
                dma("sp", y_out[r0:r0 + 128, :], xi.v().ap, xi.v().atoms, [("y", r0 // 128)], f"yst{k}")

    for l in range(L):
        layer_setup(l)
        ck(4)
        prologue(l, 0)
        ck(5)
        for t in range(NT):
            st = body(l, t, l == L - 1)
            if t + 1 < NT:
                prologue(l, t + 1)
            body2(*st)
    sc.add("sp", None, r=[("y", b) for b in range(NBT)])

    sc.finalize()
    dnames = sorted(sc.dma_count.keys())
    dsem = {n: es.enter_context(nc.semaphore("d_" + n)) for n in dnames}
    esem = {e: es.enter_context(nc.semaphore("e_" + e)) for e in ("pe", "act", "dve", "pool", "sp")}
    counts = {}
    with nc.Block() as block:
        @block.tensor
        def _(h):
            counts["pe"] = sc.emit_engine("pe", h, esem, dsem)

        @block.scalar
        def _(h):
            counts["act"] = sc.emit_engine("act", h, esem, dsem)

        @block.vector
        def _(h):
            counts["dve"] = sc.emit_engine("dve", h, esem, dsem)

        @block.gpsimd
        def _(h):
            counts["pool"] = sc.emit_engine("pool", h, esem, dsem)

        @block.sync
        def _(h):
            counts["sp"] = sc.emit_engine("sp", h, esem, dsem)
    es.close()
    kinfo = dict(counts=counts, nops=len(sc.ops), nwait=sc.nwait, sbuf_end=cur[0])
    return nc, kinfo


def _mult_table():
    s_ = np.arange(128)[:, None]
    q_ = np.arange(128)[None, :]
    slopes = np.exp2(-8.0 * (np.arange(NH, dtype=np.float64) + 1.0) / NH)
    out = np.zeros((128, NKV, 3, 2, 2, 128), np.float32)
    for j in range(3):
        dist = np.abs((s_ + (j - 1) * 128) - q_).astype(np.float64)
        valid = dist <= 128
        for kh in range(NKV):
            for hf in range(2):
                for gg in range(2):
                    h = kh * GQ + hf + 2 * gg
                    out[:, kh, j, hf, gg, :] = np.where(valid, np.exp(-slopes[h] * dist), 0.0)
    return out.reshape(128, -1)


def make_in_maps(inputs, ncores, nseq, L):
    x = np.asarray(inputs["x"], np.float32)
    B, S, _ = x.shape
    f = lambda k: np.ascontiguousarray(np.asarray(inputs[k], np.float32)[:L])
    g1 = f("ln1_g").reshape(L, 8, 128).transpose(2, 0, 1).reshape(128, L * 8)
    g2 = f("ln2_g").reshape(L, 8, 128).transpose(2, 0, 1).reshape(128, L * 8)
    gq = np.tile(f("b_q_norm_g").T, (2, 1))
    gk = np.tile(f("b_k_norm_g").T, (2, 1))
    shared = {
        "w_in": f("w_in"), "w_branch_a": f("w_branch_a"), "w_branch_b": f("w_branch_b"),
        "w_out": f("w_out"), "w_ff1": f("w_ff1"), "w_ff2": f("w_ff2"),
        "a_w_s": f("a_w_s"), "a_b_s": f("a_b_s").reshape(L, 1024), "a_norm_g": f("a_norm_g"),
        "b_sink": f("b_sink").reshape(1, L * 16),
        "g1col": np.ascontiguousarray(g1), "g2col": np.ascontiguousarray(g2),
        "gqcol": np.ascontiguousarray(gq), "gkcol": np.ascontiguousarray(gk),
        "c_ident": np.eye(128, dtype=np.float32), "c_mtab": _mult_table(),
    }
    maps = []
    for c in range(ncores):
        m = dict(shared)
        m["x"] = np.ascontiguousarray(x[c * nseq:(c + 1) * nseq].reshape(nseq * S, D))
        maps.append(m)
    return maps


def kernel(**inputs):
    x = np.asarray(inputs["x"])
    B, S, _ = x.shape
    L = np.asarray(inputs["w_in"]).shape[0]
    ncores = 8
    nseq = B // ncores
    nc, _ = build_program(nseq, S, L)
    maps = make_in_maps(inputs, ncores, nseq, L)
    res = run_bass_kernel_spmd(nc, maps, core_ids=list(range(ncores)))
    outs = [np.asarray(r["y"]).reshape(nseq, S, D) for r in res.results]
    return np.concatenate(outs, axis=0).astype(np.float32)
```
